# Optimizing a Trainium2 kernel written in Bass

```python
import jax, jax.numpy as jnp
from jax import lax
import numpy as np

D_MODEL = 1024
BATCH = 16
SEQ = 4096
DEPTH = 2
DEC_BATCH = 8
DEC_SEQ = 4096
PAST_LEN = 128

N_META = 16
BLOCK = 128
FRONT_PAD = BLOCK - N_META
D_FF = 2816
EPS = 1e-6
NEG = -1e30

A_HEADS = 8
A_DK = 64
A_DV = 64
A_THETA = 10000.0
B_HEADS = 8
B_Q_LORA = 256
B_KV_LORA = 128
B_NOPE = 64
B_ROPE = 32
B_DV = 64
B_THETA = 10000.0
C_HEADS = 16
C_KV_HEADS = 4
C_GROUP = C_HEADS // C_KV_HEADS
C_DH = 64
C_ROT = C_DH // 4
C_THETA = 500000.0
C_WINDOW = 128

EVEN_SIZES = (A_HEADS * A_DK, A_HEADS * A_DK, A_HEADS * A_DV, A_HEADS * A_DV, B_Q_LORA, B_KV_LORA, B_ROPE)
EVEN_IN = sum(EVEN_SIZES)
EVEN_MIX = A_HEADS * A_DV + B_HEADS * B_DV
ODD_IN = (C_HEADS + 2 * C_KV_HEADS) * C_DH
ODD_MIX = C_HEADS * C_DH
N_EVEN = (DEPTH + 1) // 2
N_ODD = DEPTH // 2

kernel_name = 'hybrid_retention_mla_swa_encoder'


def _rms(x, g):
    x32 = x.astype(jnp.float32)
    y = x32 * lax.rsqrt(jnp.mean(x32 * x32, axis=-1, keepdims=True) + EPS)
    return (y * g.astype(jnp.float32)).astype(x.dtype)


def _rope(x, pos, theta):
    r = x.shape[-1]
    half = r // 2
    inv = theta ** (-jnp.arange(half, dtype=jnp.float32) * 2.0 / r)
    ang = pos[:, None] * inv[None, :]
    cos = jnp.cos(ang)[:, None, :]
    sin = jnp.sin(ang)[:, None, :]
    x32 = x.astype(jnp.float32)
    x1, x2 = x32[..., :half], x32[..., half:]
    return jnp.concatenate([x1 * cos - x2 * sin, x2 * cos + x1 * sin], axis=-1).astype(x.dtype)


def _to_blocks(t):
    tp = jnp.pad(t, [(0, 0), (FRONT_PAD, 0)] + [(0, 0)] * (t.ndim - 2))
    nb = tp.shape[1] // BLOCK
    return tp.reshape((t.shape[0], nb, BLOCK) + t.shape[2:])


def _from_blocks(t):
    return t.reshape((t.shape[0], t.shape[1] * t.shape[2]) + t.shape[3:])[:, FRONT_PAD:]


def _swiglu(x, wg, wu, wd):
    return (jax.nn.silu(x @ wg) * (x @ wu)) @ wd


def _retention(q, k, v, pos, dec_f, dec_b):
    dt = q.dtype
    q = _rope(q, pos, A_THETA)
    k = _rope(k, pos, A_THETA) * (A_DK ** -0.5)
    qb, kb, vb = _to_blocks(q), _to_blocks(k), _to_blocks(v)
    lgf = -jnp.exp(dec_f.astype(jnp.float32))
    lgb = -jnp.exp(dec_b.astype(jnp.float32))
    idx = jnp.arange(BLOCK, dtype=jnp.float32)
    diff = idx[:, None] - idx[None, :]
    dmat = (jnp.where(diff >= 0, jnp.exp(lgf[:, None, None] * jnp.maximum(diff, 0.0)), 0.0)
            + jnp.where(diff < 0, jnp.exp(lgb[:, None, None] * jnp.maximum(-diff, 0.0)), 0.0))
    s = jnp.einsum('bnihd,bnjhd->bnhij', qb, kb) * dmat.astype(dt)
    o = jnp.einsum('bnhij,bnjhe->bnihe', s, vb)
    wkf = jnp.exp(lgf[None, :] * (BLOCK - 1 - idx)[:, None]).astype(dt)
    wkb = jnp.exp(lgb[None, :] * idx[:, None]).astype(dt)
    kvf = jnp.einsum('bnjhd,jh,bnjhe->bnhde', kb, wkf, vb)
    kvb = jnp.einsum('bnjhd,jh,bnjhe->bnhde', kb, wkb, vb)
    cf = jnp.exp(BLOCK * lgf)[:, None, None].astype(dt)
    cb = jnp.exp(BLOCK * lgb)[:, None, None].astype(dt)

    def fwd(state, kv):
        return cf * state + kv, state

    def bwd(state, kv):
        return cb * state + kv, state

    init = jnp.zeros(kvf.shape[:1] + kvf.shape[2:], dt)
    _, sf = lax.scan(fwd, init, jnp.moveaxis(kvf, 1, 0))
    _, sb = lax.scan(bwd, init, jnp.moveaxis(kvb, 1, 0), reverse=True)
    sf = jnp.moveaxis(sf, 0, 1)
    sb = jnp.moveaxis(sb, 0, 1)
    qf = qb * jnp.exp(lgf[None, :] * (idx + 1.0)[:, None]).astype(dt)[:, :, None]
    qbk = qb * jnp.exp(lgb[None, :] * (BLOCK - idx)[:, None]).astype(dt)[:, :, None]
    o = o + jnp.einsum('bnihd,bnhde->bnihe', qf, sf) + jnp.einsum('bnihd,bnhde->bnihe', qbk, sb)
    return _from_blocks(o)


def _mla(cq, ckv, kr, pos, q_norm, w_qb, kv_norm, w_kvb, gq, gk):
    b, l, _ = cq.shape
    q = (_rms(cq, q_norm) @ w_qb).reshape(b, l, B_HEADS, B_NOPE + B_ROPE)
    kv = (_rms(ckv, kv_norm) @ w_kvb).reshape(b, l, B_HEADS, B_NOPE + B_DV)
    k_nope, v = kv[..., :B_NOPE], kv[..., B_NOPE:]
    k = jnp.concatenate([k_nope, jnp.broadcast_to(kr[:, :, None, :], (b, l, B_HEADS, B_ROPE))], axis=-1)
    q = _rms(q, gq)
    k = _rms(k, gk)
    q = jnp.concatenate([q[..., :B_NOPE], _rope(q[..., B_NOPE:], pos, B_THETA)], axis=-1)
    k = jnp.concatenate([k[..., :B_NOPE], _rope(k[..., B_NOPE:], pos, B_THETA)], axis=-1)
    scale = (B_NOPE + B_ROPE) ** -0.5
    dt = q.dtype
    qb = jnp.moveaxis(_to_blocks(q), 1, 0)

    def blk(qn):
        s = jnp.einsum('bqhd,bkhd->bhqk', qn, k).astype(jnp.float32) * scale
        p = jax.nn.softmax(s, axis=-1).astype(dt)
        return jnp.einsum('bhqk,bkhe->bqhe', p, v)

    o = lax.map(blk, qb)
    return _from_blocks(jnp.moveaxis(o, 0, 1)).reshape(b, l, B_HEADS * B_DV)


def _even_mixer(hn, pos, w_in, dec_f, dec_b, ret_norm, q_norm, w_qb, kv_norm, w_kvb, gq, gk, w_out):
    b, l, _ = hn.shape
    z = hn @ w_in
    cuts = [int(c) for c in np.cumsum(EVEN_SIZES)[:-1]]
    qa, ka, va, ga, cq, ckv, kr = jnp.split(z, cuts, axis=-1)
    ret = _retention(qa.reshape(b, l, A_HEADS, A_DK), ka.reshape(b, l, A_HEADS, A_DK),
                     va.reshape(b, l, A_HEADS, A_DV), pos, dec_f, dec_b)
    ret = _rms(ret, ret_norm) * jax.nn.silu(ga.reshape(b, l, A_HEADS, A_DV))
    mla = _mla(cq, ckv, kr, pos, q_norm, w_qb, kv_norm, w_kvb, gq, gk)
    return jnp.concatenate([ret.reshape(b, l, A_HEADS * A_DV), mla], axis=-1) @ w_out


def _band(t, nb):
    b = t.shape[0]
    tp = jnp.pad(t, [(0, 0), (FRONT_PAD + BLOCK, BLOCK), (0, 0), (0, 0)])
    tb = tp.reshape((b, nb + 2, BLOCK) + t.shape[2:])
    nbr = jnp.concatenate([tb[:, :-2], tb[:, 1:-1], tb[:, 2:]], axis=2)
    return jnp.moveaxis(nbr, 1, 0)


def _window_mixer(hn, pos, w_in, gq, gk, sink, w_out):
    b, l, _ = hn.shape
    z = hn @ w_in
    q, k, v = jnp.split(z, [C_HEADS * C_DH, (C_HEADS + C_KV_HEADS) * C_DH], axis=-1)
    q = _rms(q.reshape(b, l, C_HEADS, C_DH), gq)
    k = _rms(k.reshape(b, l, C_KV_HEADS, C_DH), gk)
    v = v.reshape(b, l, C_KV_HEADS, C_DH)
    q = jnp.concatenate([_rope(q[..., :C_ROT], pos, C_THETA), q[..., C_ROT:]], axis=-1)
    k = jnp.concatenate([_rope(k[..., :C_ROT], pos, C_THETA), k[..., C_ROT:]], axis=-1)
    dt = q.dtype
    mk, mv = k[:, :N_META], v[:, :N_META]
    qblk = _to_blocks(q)
    nb = qblk.shape[1]
    qb = jnp.moveaxis(qblk.reshape(b, nb, BLOCK, C_KV_HEADS, C_GROUP, C_DH), 1, 0)
    kn, vn = _band(k, nb), _band(v, nb)
    n = jnp.arange(nb, dtype=jnp.int32)[:, None]
    q_pos = n * BLOCK + jnp.arange(BLOCK, dtype=jnp.int32)[None, :] - FRONT_PAD
    k_pos = (n - 1) * BLOCK + jnp.arange(3 * BLOCK, dtype=jnp.int32)[None, :] - FRONT_PAD
    valid = ((k_pos[:, None, :] >= N_META) & (k_pos[:, None, :] < l)
             & (jnp.abs(q_pos[:, :, None] - k_pos[:, None, :]) <= C_WINDOW))
    mask = jnp.concatenate([jnp.ones((nb, BLOCK, N_META), dtype=bool), valid], axis=-1)
    sink_l = sink.astype(jnp.float32).reshape(C_KV_HEADS, C_GROUP)[None, :, :, None, None]
    scale = C_DH ** -0.5

    def blk(args):
        qn, kb, vb, m = args
        kall = jnp.concatenate([mk, kb], axis=1)
        vall = jnp.concatenate([mv, vb], axis=1)
        s = jnp.einsum('bqkgd,bjkd->bkgqj', qn, kall).astype(jnp.float32) * scale
        s = jnp.where(m[None, None, None], s, NEG)
        s = jnp.concatenate([s, jnp.broadcast_to(sink_l, s.shape[:-1] + (1,))], axis=-1)
        p = jax.nn.softmax(s, axis=-1)[..., :-1].astype(dt)
        return jnp.einsum('bkgqj,bjkd->bqkgd', p, vall)

    o = lax.map(blk, (qb, kn, vn, mask))
    o = _from_blocks(jnp.moveaxis(o, 0, 1)).reshape(b, l, ODD_MIX)
    return o @ w_out


def _trunk(x, meta_tokens, ffn_norm, ffn_w_gate, ffn_w_up, ffn_w_down, mix_norm,
           even_w_in, ret_decay_f, ret_decay_b, ret_out_norm, mla_q_norm, mla_w_qb,
           mla_kv_norm, mla_w_kvb, mla_qk_norm_q, mla_qk_norm_k, even_w_out,
           odd_w_in, swa_q_norm, swa_k_norm, swa_sink, odd_w_out):
    b = x.shape[0]
    h = jnp.concatenate([jnp.broadcast_to(meta_tokens[None].astype(x.dtype), (b, N_META, D_MODEL)), x], axis=1)
    pos = jnp.arange(h.shape[1], dtype=jnp.float32)
    for layer in range(DEPTH):
        h = h + 0.5 * _swiglu(_rms(h, ffn_norm[layer, 0]), ffn_w_gate[layer, 0], ffn_w_up[layer, 0], ffn_w_down[layer, 0])
        hn = _rms(h, mix_norm[layer])
        i = layer // 2
        if layer % 2 == 0:
            h = h + _even_mixer(hn, pos, even_w_in[i], ret_decay_f[i], ret_decay_b[i], ret_out_norm[i],
                                mla_q_norm[i], mla_w_qb[i], mla_kv_norm[i], mla_w_kvb[i],
                                mla_qk_norm_q[i], mla_qk_norm_k[i], even_w_out[i])
        else:
            h = h + _window_mixer(hn, pos, odd_w_in[i], swa_q_norm[i], swa_k_norm[i], swa_sink[i], odd_w_out[i])
        h = h + 0.5 * _swiglu(_rms(h, ffn_norm[layer, 1]), ffn_w_gate[layer, 1], ffn_w_up[layer, 1], ffn_w_down[layer, 1])
    return h[:, N_META:]


def setup_inputs(seed: int = 0) -> dict:
    key = jax.random.key(seed)
    ks = iter(jax.random.split(key, 32))
    f32 = jnp.float32

    def w(shape, fan_in):
        return jax.random.normal(next(ks), shape, f32) * (fan_in ** -0.5)

    def gain(shape):
        return 1.0 + 0.02 * jax.random.normal(next(ks), shape, f32)

    base = jnp.log(-jnp.log1p(-(2.0 ** (-5.0 - jnp.arange(A_HEADS, dtype=f32)))))
    return {
        'x_prompt': jax.random.normal(next(ks), (BATCH, SEQ, D_MODEL), f32),
        'x_sample': jax.random.normal(next(ks), (DEC_BATCH, DEC_SEQ, D_MODEL), f32),
        'meta_tokens': jax.random.normal(next(ks), (N_META, D_MODEL), f32),
        'ffn_norm': gain((DEPTH, 2, D_MODEL)),
        'ffn_w_gate': w((DEPTH, 2, D_MODEL, D_FF), D_MODEL),
        'ffn_w_up': w((DEPTH, 2, D_MODEL, D_FF), D_MODEL),
        'ffn_w_down': w((DEPTH, 2, D_FF, D_MODEL), D_FF),
        'mix_norm': gain((DEPTH, D_MODEL)),
        'even_w_in': w((N_EVEN, D_MODEL, EVEN_IN), D_MODEL),
        'ret_decay_f': base[None] + 0.1 * jax.random.normal(next(ks), (N_EVEN, A_HEADS), f32),
        'ret_decay_b': base[None] + 0.1 * jax.random.normal(next(ks), (N_EVEN, A_HEADS), f32),
        'ret_out_norm': gain((N_EVEN, A_HEADS, A_DV)),
        'mla_q_norm': gain((N_EVEN, B_Q_LORA)),
        'mla_w_qb': w((N_EVEN, B_Q_LORA, B_HEADS * (B_NOPE + B_ROPE)), B_Q_LORA),
        'mla_kv_norm': gain((N_EVEN, B_KV_LORA)),
        'mla_w_kvb': w((N_EVEN, B_KV_LORA, B_HEADS * (B_NOPE + B_DV)), B_KV_LORA),
        'mla_qk_norm_q': gain((N_EVEN, B_NOPE + B_ROPE)),
        'mla_qk_norm_k': gain((N_EVEN, B_NOPE + B_ROPE)),
        'even_w_out': w((N_EVEN, EVEN_MIX, D_MODEL), EVEN_MIX),
        'odd_w_in': w((N_ODD, D_MODEL, ODD_IN), D_MODEL),
        'swa_q_norm': gain((N_ODD, C_DH)),
        'swa_k_norm': gain((N_ODD, C_DH)),
        'swa_sink': 0.5 * jax.random.normal(next(ks), (N_ODD, C_HEADS), f32),
        'odd_w_out': w((N_ODD, ODD_MIX, D_MODEL), ODD_MIX),
    }


def reference(x_prompt, x_sample, meta_tokens, ffn_norm, ffn_w_gate, ffn_w_up, ffn_w_down, mix_norm,
              even_w_in, ret_decay_f, ret_decay_b, ret_out_norm, mla_q_norm, mla_w_qb, mla_kv_norm,
              mla_w_kvb, mla_qk_norm_q, mla_qk_norm_k, even_w_out, odd_w_in, swa_q_norm, swa_k_norm,
              swa_sink, odd_w_out):
    params = (meta_tokens, ffn_norm, ffn_w_gate, ffn_w_up, ffn_w_down, mix_norm,
              even_w_in, ret_decay_f, ret_decay_b, ret_out_norm, mla_q_norm, mla_w_qb,
              mla_kv_norm, mla_w_kvb, mla_qk_norm_q, mla_qk_norm_k, even_w_out,
              odd_w_in, swa_q_norm, swa_k_norm, swa_sink, odd_w_out)
    y_prompt = _trunk(x_prompt, *params)
    y_sample = _trunk(x_sample, *params)
    return (y_prompt, y_sample)
```

```python
import numpy as np
from contextlib import ExitStack
import concourse.bass as bass
import concourse.mybir as mybir
from concourse.bass_utils import run_bass_kernel_spmd

F32 = mybir.dt.float32
BF16 = mybir.dt.bfloat16
ALU = mybir.AluOpType
AF = mybir.ActivationFunctionType
AX = mybir.AxisListType

ATTACH_WAIT = True
ENGS = ("sync", "tensor", "vector", "scalar", "gpsimd")

D = 1024
DFF = 2816
NFC = DFF // 128
NMETA = 16
EPS = 1e-6


class T:
    __slots__ = ("name", "w", "r", "dsem", "dcount", "excl")

    def __init__(self, name, excl=False):
        self.name = name
        self.excl = excl
        self.w = None
        self.r = []
        self.dsem = {}
        self.dcount = {}


class Op:
    __slots__ = ("eng", "fn", "deps", "is_dma", "sem", "val", "signal")

    def __init__(self, eng, fn, deps, is_dma):
        self.eng = eng
        self.fn = fn
        self.deps = deps
        self.is_dma = is_dma
        self.sem = None
        self.val = None
        self.signal = is_dma


import types


def _freeze(fn, depth=0):
    if not isinstance(fn, types.FunctionType) or fn.__closure__ is None or depth > 3:
        return fn
    cells = []
    for c in fn.__closure__:
        try:
            v = c.cell_contents
        except ValueError:
            cells.append(c)
            continue
        if isinstance(v, types.FunctionType) and v.__closure__ is not None:
            v = _freeze(v, depth + 1)
        cells.append(types.CellType(v))
    g = types.FunctionType(fn.__code__, fn.__globals__, fn.__name__, fn.__defaults__, tuple(cells))
    g.__kwdefaults__ = fn.__kwdefaults__
    return g


class Prog:
    def __init__(self, nc, stack):
        self.nc = nc
        self.stack = stack
        self.q = {e: [] for e in ENGS}
        self.esem = {e: stack.enter_context(nc.semaphore("es_" + e)) for e in ENGS}
        self.nsem = len(ENGS)
        self.pending_dma = []
        self.sem_pool = {}

    def new_sem(self, name):
        self.nsem += 1
        return self.stack.enter_context(self.nc.semaphore(name))

    def op(self, eng, fn, reads=(), writes=(), dma_T=None, npieces=1):
        if any(t.excl for t in reads):
            writes = list(writes) + [t for t in reads if t.excl and t not in writes]
            reads = [t for t in reads if not t.excl]
        deps = []
        for t in reads:
            if t.w is not None:
                deps.append((t.w, "raw"))
        for t in writes:
            if t.w is not None:
                deps.append((t.w, "waw"))
            for o in t.r:
                deps.append((o, "war"))
        is_dma = dma_T is not None
        o = Op(eng, _freeze(fn), deps, is_dma)
        if is_dma:
            if eng not in dma_T.dsem:
                pool = self.sem_pool.setdefault(eng, [])
                if pool:
                    dma_T.dsem[eng], dma_T.dcount[eng] = pool.pop()
                else:
                    dma_T.dsem[eng] = self.new_sem("d%d_%s_%s" % (self.nsem, dma_T.name, eng[:2]))
                    dma_T.dcount[eng] = 0
            dma_T.dcount[eng] += 16 * npieces
            o.sem = dma_T.dsem[eng]
            o.val = dma_T.dcount[eng]
            self.pending_dma.append(o)
        for t in writes:
            t.w = o
            t.r = []
        for t in reads:
            t.r.append(o)
        self.q[eng].append(o)
        return o

    def recycle(self, t):
        for eng, sem in t.dsem.items():
            self.sem_pool.setdefault(eng, []).append((sem, t.dcount[eng]))
        t.dsem = {}
        t.dcount = {}

    def barrier(self):
        last = []
        for e in ENGS:
            for o in reversed(self.q[e]):
                if not o.is_dma and o.fn is not None:
                    last.append(o)
                    break
        deps = [(o, "raw") for o in last + self.pending_dma]
        self.pending_dma = []
        for e in ENGS:
            self.q[e].append(Op(e, None, list(deps), False))

    def finalize(self):
        nc = self.nc
        for e in ENGS:
            for o in self.q[e]:
                for (d, kind) in o.deps:
                    if d.is_dma:
                        continue
                    if d.eng == o.eng and o.eng in ("tensor", "sync"):
                        continue
                    d.signal = True
        for e in ENGS:
            c = 0
            for o in self.q[e]:
                if o.is_dma or o.fn is None:
                    continue
                if o.signal:
                    c += 1
                    o.sem = self.esem[e]
                    o.val = c
        self.counts = {}
        with nc.Block() as block:
            def replay(engname):
                def run(eng):
                    waited = {}
                    n_w = 0
                    for o in self.q[engname]:
                        need = {}
                        for (d, kind) in o.deps:
                            if d.sem is None:
                                continue
                            if (not d.is_dma) and d.eng == engname:
                                if engname in ("tensor", "sync"):
                                    continue
                            k = id(d.sem)
                            if waited.get(k, 0) >= d.val:
                                continue
                            if k not in need or need[k][1] < d.val:
                                need[k] = (d.sem, d.val)
                        items = list(need.items())
                        attach = None
                        if ATTACH_WAIT and o.fn is not None and not o.is_dma and items:
                            attach = items.pop()
                        for k, (s, v) in items:
                            eng.wait_ge(s, v)
                            waited[k] = v
                            n_w += 1
                        if o.fn is None:
                            continue
                        n0 = nc.n_instructions()
                        ins = o.fn(eng)
                        if attach is not None:
                            k, (s, v) = attach
                            ins._wait_ge(s, v)
                            waited[k] = v
                        if o.is_dma:
                            n1 = nc.n_instructions()
                            exp = len(ins) if isinstance(ins, (list, tuple)) else 1
                            if n1 - n0 != exp:
                                print("DMA SPLIT", engname, n1 - n0, exp, o.fn.__code__.co_firstlineno)
                            if not isinstance(ins, (list, tuple)):
                                ins = [ins]
                            for i_ in ins:
                                i_.then_inc(o.sem, 16)
                        elif o.signal:
                            ins.then_inc(o.sem, 1)
                    self.counts[engname] = (len(self.q[engname]), n_w)
                return run
            block.sync(replay("sync"))
            block.tensor(replay("tensor"))
            block.vector(replay("vector"))
            block.scalar(replay("scalar"))
            block.gpsimd(replay("gpsimd"))


class Arena:
    def __init__(self, ap, words):
        self.ap = ap
        self.words = words
        self.off = 0
        self.peak = 0
        self.tl = []

    def f32(self, name, cols):
        a = self.ap[:, self.off:self.off + cols]
        t = T(name)
        self.tl.append((self.off, t))
        self.off += cols
        self.peak = max(self.peak, self.off)
        assert self.off <= self.words, ("SBUF arena overflow", name, self.off, self.words)
        return a, t

    def bf16(self, name, cols):
        assert cols % 2 == 0
        a, t = self.f32(name, cols // 2)
        return a.bitcast(BF16), t

    def mark(self):
        return self.off

    def release(self, m, P=None):
        self.off = m
        while self.tl and self.tl[-1][0] >= m:
            _, t = self.tl.pop()
            if P is not None:
                P.recycle(t)


class Ring:
    def __init__(self, P, slots, eng="sync"):
        self.P = P
        self.slots = slots
        self.plan = []
        self.issued = 0
        self.consumed = 0
        self.released = 0
        self.eng = eng

    def extend(self, loads):
        self.plan.extend(loads)

    def _fill(self):
        n = len(self.slots)
        while self.issued < len(self.plan) and self.issued < self.released + n:
            ap, t = self.slots[self.issued % n]
            pieces = self.plan[self.issued](ap)
            self.P.op(self.eng, lambda e, pieces=pieces: [e.dma_start(out=o, in_=s) for (o, s) in pieces],
                      writes=[t], dma_T=t, npieces=len(pieces))
            self.issued += 1

    def next(self):
        self._fill()
        assert self.consumed < self.issued, "ring underflow"
        s = self.slots[self.consumed % len(self.slots)]
        self.consumed += 1
        return s

    def release(self):
        self.released += 1
        self._fill()


QORDER = [0, 4, 1, 5, 2, 6, 3, 7, 8, 12, 9, 13, 10, 14, 11, 15]
EVEN_IN = 2464
NCONST = 836
NROPE = 176
VS = 96


class Cfg:
    def __init__(self, S=4096, NSEQ=3, upto=99, tile_sub=4):
        self.S = S
        self.NSEQ = NSEQ
        self.upto = upto
        self.L = S + NMETA
        self.NB = S // 128
        self.NBLK = self.NB + 1
        self.tile_sub = tile_sub
        tiles = [(0, [NMETA])]
        b = 0
        while b < self.NB:
            n = min(tile_sub, self.NB - b)
            tiles.append((NMETA + 128 * b, [128] * n))
            b += n
        self.tiles = tiles


def host_constants(L):
    p = np.arange(128, dtype=np.float32)[:, None]
    i = np.arange(128, dtype=np.float32)[None, :]
    c = np.zeros((128, NCONST), np.float32)
    c[:, 0] = 127 - p[:, 0]
    c[:, 1] = p[:, 0]
    c[:, 2] = p[:, 0] + 1
    c[:, 3] = 128 - p[:, 0]
    c[:, 4:132] = i + 1
    c[:, 132:260] = 128 - i
    c[:, 260:324] = 128.0
    c[:, 324:452] = np.maximum(i - p, 0)
    c[:, 452:580] = np.maximum(p - i, 0)
    c[:, 580:708] = (p >= i)
    c[:, 708:836] = (p <= i)
    pm = (112 + np.arange(16, dtype=np.float32))
    cm = np.stack([127 - pm, pm, pm + 1, 128 - pm], axis=1).astype(np.float32)
    pos = np.arange(L, dtype=np.float32)

    def cs(r, theta):
        half = r // 2
        inv = (np.float32(theta) ** (-np.arange(half, dtype=np.float32) * np.float32(2.0) / np.float32(r))).astype(np.float32)
        ang = (pos[:, None] * inv[None, :]).astype(np.float32)
        return np.cos(ang).astype(np.float32), np.sin(ang).astype(np.float32)

    ca, sa = cs(64, 10000.0)
    cb, sb = cs(32, 10000.0)
    cc, sc = cs(16, 500000.0)
    rope = np.concatenate([ca, sa, ca * 0.125, sa * 0.125, cb, sb, cc, sc], axis=1).astype(np.float32)
    assert rope.shape[1] == NROPE
    return c, cm, rope


def build(cfg):
    import os
    DBG = float(os.environ.get('DBG', '99'))
    nc = bass.Bass("TRN2", target_bir_lowering=False)
    S, NSEQ, L, NB, NBLK, upto = cfg.S, cfg.NSEQ, cfg.L, cfg.NB, cfg.NBLK, cfg.upto

    def din(name, shape):
        return nc.dram_tensor(name, list(shape), F32, kind="ExternalInput").ap()

    x_in = din("x", [NSEQ, S, D])
    meta_in = din("meta_tokens", [NMETA, D])
    ffn_norm = din("ffn_norm", [2, 2, D])
    w_gate = din("ffn_w_gate", [2, 2, D, DFF])
    w_up = din("ffn_w_up", [2, 2, D, DFF])
    w_down = din("ffn_w_down", [2, 2, DFF, D])
    mix_norm = din("mix_norm", [2, D])
    even_w_in = din("even_w_in", [1, D, EVEN_IN])
    dec_f = din("ret_decay_f", [1, 8])
    dec_b = din("ret_decay_b", [1, 8])
    ret_out_norm = din("ret_out_norm", [1, 8, 64])
    mla_q_norm = din("mla_q_norm", [1, 256])
    mla_w_qb = din("mla_w_qb", [1, 256, 768])
    mla_kv_norm = din("mla_kv_norm", [1, 128])
    mla_w_kvb = din("mla_w_kvb", [1, 128, 1024])
    mla_gq = din("mla_qk_norm_q", [1, 96])
    mla_gk = din("mla_qk_norm_k", [1, 96])
    even_w_out = din("even_w_out", [1, 1024, 1024])
    odd_w_in = din("odd_w_in", [1, D, 1536])
    swa_gq = din("swa_q_norm", [1, 64])
    swa_gk = din("swa_k_norm", [1, 64])
    swa_sink = din("swa_sink", [1, 16])
    odd_w_out = din("odd_w_out", [1, 1024, 1024])
    CONST = din("CONST", [128, NCONST])
    CONSTM = din("CONSTM", [16, 4])
    ROPE = din("ROPE", [L, NROPE])
    y_out = nc.dram_tensor("y", [NSEQ, S, D], F32, kind="ExternalOutput").ap()

    def scratch(name, shape, dt):
        return nc.dram_tensor(name, list(shape), dt).ap()

    WGU = scratch("WGU", [4, NFC, 128, 2, 8, 128], BF16)
    WD = scratch("WD", [4, 2, 128, NFC, 512], BF16)
    EWIN = scratch("EWIN", [128, 8, EVEN_IN], BF16)
    OWIN = scratch("OWIN", [128, 8, 1536], BF16)
    EWOUT = scratch("EWOUT", [128, 8, 1024], BF16)
    OWOUT = scratch("OWOUT", [128, 8, 1024], BF16)
    WQB = scratch("WQB", [128, 3, 1024], BF16)
    H1 = scratch("H1", [NSEQ, L, D], F32)
    H2 = scratch("H2", [NSEQ, L, D], F32)
    H3 = scratch("H3", [NSEQ, L, D], F32)
    H4 = scratch("H4", [NSEQ, L, D], F32)
    RQT = scratch("RQT", [NSEQ, 4, 128, L], BF16)
    RKT = scratch("RKT", [NSEQ, 4, 128, L], BF16)
    RV = scratch("RV", [NSEQ, L, 512], BF16)
    RG = scratch("RG", [NSEQ, L, 512], F32)
    KVS = scratch("KVS", [NSEQ, NBLK, 128, 512], F32)
    SFF = scratch("SFF", [NSEQ, NBLK, 128, 256], BF16)
    SFB = scratch("SFB", [NSEQ, NBLK, 128, 256], BF16)
    MQT = scratch("MQT", [NSEQ, 8, 96, L], BF16)
    MKT = scratch("MKT", [NSEQ, 8, 96, L], BF16)
    MV = scratch("MV", [NSEQ, 128, NBLK, 8, VS], BF16)
    MIXM = scratch("MIXM", [NSEQ, L, 512], BF16)
    CQT = scratch("CQT", [NSEQ, 8, 128, L], BF16)
    CKT = scratch("CKT", [NSEQ, 2, 128, L], BF16)
    CV = scratch("CV", [NSEQ, 128, NBLK, 4, VS], BF16)
    TD = {}

    def dT(name):
        if name not in TD:
            TD[name] = T(name)
        return TD[name]

    with ExitStack() as st:
        AW = 51200
        arena_ap = st.enter_context(nc.sbuf_tensor("arena", [128, AW], F32))
        A = Arena(arena_ap, AW)
        pbank = [st.enter_context(nc.psum_tensor("pb%d" % i, [128, 512], F32)) for i in range(8)]
        Tpb = [T("pb%d" % i, excl=True) for i in range(8)]
        P = Prog(nc, st)

        def pb16(b):
            return pbank[b].bitcast(BF16).rearrange("p (k t) -> p k t", t=128)

        def V(fn, reads=(), writes=()):
            return P.op("vector", fn, reads, writes)

        def SC(fn, reads=(), writes=()):
            return P.op("scalar", fn, reads, writes)

        def G(fn, reads=(), writes=()):
            return P.op("gpsimd", fn, reads, writes)

        def PE(fn, reads=(), writes=()):
            return P.op("tensor", fn, reads, writes)

        def LD(out_ap, in_ap, t_out, reads=(), slow=False):
            return P.op("sync", lambda e: e.dma_start(out=out_ap, in_=in_ap, allow_slow_non_contiguous=slow),
                        reads=reads, writes=[t_out], dma_T=t_out)

        def ST(out_ap, in_ap, t_in, t_out, slow=False):
            return P.op("gpsimd", lambda e: e.dma_start(out=out_ap, in_=in_ap, allow_slow_non_contiguous=slow),
                        reads=[t_in], writes=[t_out], dma_T=t_in)

        def rsqrt_chain(ap, t, mul, eps=EPS):
            V(lambda e: e.tensor_scalar(out=ap, in0=ap, scalar1=mul, scalar2=eps, op0=ALU.mult, op1=ALU.add), [t], [t])
            SC(lambda e: e.activation(out=ap, in_=ap, func=AF.Sqrt), [t], [t])
            V(lambda e: e.reciprocal(out=ap, in_=ap), [t], [t])

        ident, T_ident = A.bf16("ident", 128)
        cst, T_cst = A.f32("cst", NCONST)
        cstm, T_cstm = A.f32("cstm", 4)
        LD(cst, CONST, T_cst)
        LD(cstm[:16, :], CONSTM, T_cstm)
        IDX = cst[:, 0:4]
        IROW = cst[:, 4:260].rearrange("p (f i) -> p f i", f=2)
        C128 = cst[:, 260:324]
        AM = cst[:, 324:452]
        BM = cst[:, 452:580]
        m0 = A.mark()
        identf, T_identf = A.f32("identf", 128)
        G(lambda e: e.memset(identf, 0.0), [], [T_identf])
        G(lambda e: e.affine_select(out=identf, in_=identf, compare_op=ALU.not_equal, fill=1.0, base=0,
                                    pattern=[[-1, 128]], channel_multiplier=1), [T_identf], [T_identf])
        V(lambda e: e.tensor_copy(out=ident, in_=identf), [T_identf], [T_ident])
        mask16, T_mask = A.bf16("mask16", 256)
        V(lambda e: e.tensor_copy(out=mask16, in_=cst[:, 580:836]), [T_cst], [T_mask])
        MPREV = mask16[:, 0:128]
        MNEXT = mask16[:, 128:256]

        gall, T_gall = A.f32("gall", 56)
        for i in range(4):
            LD(gall[:, i * 8:(i + 1) * 8], ffn_norm[i // 2, i % 2, :].rearrange("(k p) -> p k", p=128), T_gall, slow=True)
        for i in range(2):
            LD(gall[:, 32 + i * 8:32 + (i + 1) * 8], mix_norm[i, :].rearrange("(k p) -> p k", p=128), T_gall, slow=True)
        LD(gall[:, 48:52], ret_out_norm[0].rearrange("h e -> (h e)").rearrange("(k p) -> p k", p=128), T_gall, slow=True)
        LD(gall[:, 52:54], mla_q_norm[0].rearrange("(k p) -> p k", p=128), T_gall, slow=True)
        LD(gall[:, 54:55], mla_kv_norm[0].rearrange("(k p) -> p k", p=128), T_gall, slow=True)

        lg, T_lg = A.f32("lg", 16)
        lg3 = lg.rearrange("p (f h) -> p f h", f=2)
        lgP, T_lgP = A.f32("lgP", 8)
        lgP3 = lgP.rearrange("p (f c) -> p f c", f=2)
        for f, dec in enumerate((dec_f, dec_b)):
            LD(lg3[:, f, :], dec[0].partition_broadcast(128), T_lg, slow=True)
            dv = dec[0].rearrange("(c b) -> b c", b=2)
            LD(lgP3[0:64, f, :], dv[0].partition_broadcast(64), T_lgP, slow=True)
            LD(lgP3[64:128, f, :], dv[1].partition_broadcast(64), T_lgP, slow=True)
        for (ap, t) in ((lg, T_lg), (lgP, T_lgP)):
            SC(lambda e, ap=ap: e.activation(out=ap, in_=ap, func=AF.Exp), [t], [t])
            V(lambda e, ap=ap: e.tensor_scalar(out=ap, in0=ap, scalar1=-1.0, scalar2=None, op0=ALU.mult), [t], [t])
        wk, T_wk = A.f32("wk", 16)
        wk3 = wk.rearrange("p (f h) -> p f h", f=2)
        wkM, T_wkM = A.f32("wkM", 16)
        wkM3 = wkM.rearrange("p (f h) -> p f h", f=2)
        for f in range(2):
            SC(lambda e, f=f: e.activation(out=wk3[:, f, :], in_=lg3[:, f, :], func=AF.Exp, scale=IDX[:, f:f + 1]),
               [T_lg, T_cst], [T_wk])
            SC(lambda e, f=f: e.activation(out=wkM3[:16, f, :], in_=lg3[:16, f, :], func=AF.Exp, scale=cstm[:16, f:f + 1]),
               [T_lg, T_cstm], [T_wkM])
        qdec, T_qdec = A.f32("qdec", 4 * 2 * 128)
        qdec4 = qdec.rearrange("p (c f i) -> p c f i", c=4, f=2)
        cfb, T_cfb = A.f32("cfb", 4 * 2 * 64)
        cfb4 = cfb.rearrange("p (c f e) -> p c f e", c=4, f=2)
        for c in range(4):
            for f in range(2):
                SC(lambda e, c=c, f=f: e.activation(out=qdec4[:, c, f, :], in_=IROW[:, f, :], func=AF.Exp,
                                                    scale=lgP3[:, f, c:c + 1]), [T_lgP, T_cst], [T_qdec])
                SC(lambda e, c=c, f=f: e.activation(out=cfb4[:, c, f, :], in_=C128, func=AF.Exp,
                                                    scale=lgP3[:, f, c:c + 1]), [T_lgP, T_cst], [T_cfb])
        DTt, T_DT = A.f32("DT", 8 * 128)
        DT3 = DTt.rearrange("p (h i) -> p h i", h=8)
        etmp, T_etmp = A.f32("etmp", 128)
        for h in range(8):
            SC(lambda e, h=h: e.activation(out=DT3[:, h, :], in_=AM, func=AF.Exp, scale=lg3[:, 0, h:h + 1]),
               [T_lg, T_cst], [T_DT])
            SC(lambda e, h=h: e.activation(out=etmp, in_=BM, func=AF.Exp, scale=lg3[:, 1, h:h + 1]),
               [T_lg, T_cst], [T_etmp])
            V(lambda e, h=h: e.tensor_tensor(out=DT3[:, h, :], in0=DT3[:, h, :], in1=etmp, op=ALU.mult),
              [T_DT, T_etmp], [T_DT])
        gqk, T_gqk = A.f32("gqk", 192)
        gqk3 = gqk.rearrange("p (a d) -> p a d", a=2)
        LD(gqk3[:, 0, :], mla_gq[0].partition_broadcast(128), T_gqk, slow=True)
        LD(gqk3[:, 1, :], mla_gk[0].partition_broadcast(128), T_gqk, slow=True)
        V(lambda e: e.tensor_scalar(out=gqk3[:, 0, :], in0=gqk3[:, 0, :], scalar1=float(96 ** -0.5), scalar2=None,
                                    op0=ALU.mult), [T_gqk], [T_gqk])
        gck, T_gck = A.f32("gck", 20 * 64)
        gck3 = gck.rearrange("p (h d) -> p h d", d=64)
        gtmp, T_gtmp = A.f32("gtmp", 128)
        LD(gtmp[:, 0:64], swa_gq[0].partition_broadcast(128), T_gtmp, slow=True)
        LD(gtmp[:, 64:128], swa_gk[0].partition_broadcast(128), T_gtmp, slow=True)
        V(lambda e: e.tensor_scalar(out=gck3[:, 0:16, :], in0=gtmp[:, 0:64].unsqueeze(1).to_broadcast([128, 16, 64]),
                                    scalar1=0.125, scalar2=None, op0=ALU.mult), [T_gtmp], [T_gck])
        V(lambda e: e.tensor_copy(out=gck3[:, 16:20, :], in_=gtmp[:, 64:128].unsqueeze(1).to_broadcast([128, 4, 64])),
          [T_gtmp], [T_gck])
        esink, T_esink = A.f32("esink", 16)
        es4 = esink.rearrange("p (B m b) -> p B m b", B=2, m=4)
        for B_ in range(2):
            for b_ in range(2):
                LD(es4[:, B_, :, b_], swa_sink[0, B_ * 8 + 4 * b_:B_ * 8 + 4 * b_ + 4].partition_broadcast(128), T_esink, slow=True)
        SC(lambda e: e.activation(out=esink, in_=esink, func=AF.Exp), [T_esink], [T_esink])
        wqb, T_wqb = A.bf16("wqb", 3 * 1024)
        wqb3 = wqb.rearrange("p (k n) -> p k n", k=3)
        wkvb, T_wkvb = wqb3[:, 2, :], T_wqb
        mC = A.mark()

        NST = 3
        stg32 = [A.f32("stg32_%d" % i, DFF) for i in range(NST)]
        stg16 = [A.bf16("stg16_%d" % i, DFF) for i in range(NST)]
        prep_i = [0]

        def prep_rows(src_pieces, ncols, gain_ap, dst_pieces, T_dst):
            k = prep_i[0] % NST
            prep_i[0] += 1
            s32, t32 = stg32[k]
            s16, t16 = stg16[k]
            P.op("sync", lambda e: [e.dma_start(out=s32[p0:p1, :ncols], in_=ap) for (p0, p1, ap) in src_pieces],
                 writes=[t32], dma_T=t32, npieces=len(src_pieces))
            eng = "scalar" if (prep_i[0] % 2 == 0) else "vector"
            if gain_ap is None:
                if eng == "scalar":
                    SC(lambda e: e.copy(out=s16[:, :ncols], in_=s32[:, :ncols]), [t32], [t16])
                else:
                    V(lambda e: e.tensor_copy(out=s16[:, :ncols], in_=s32[:, :ncols]), [t32], [t16])
            else:
                if eng == "scalar":
                    SC(lambda e: e.activation(out=s16[:, :ncols], in_=s32[:, :ncols], func=AF.Copy, scale=gain_ap),
                       [t32, T_gall], [t16])
                else:
                    V(lambda e: e.tensor_scalar(out=s16[:, :ncols], in0=s32[:, :ncols], scalar1=gain_ap, scalar2=None,
                                                op0=ALU.mult), [t32, T_gall], [t16])
            P.op("gpsimd", lambda e: [e.dma_start(out=d, in_=(s16[:, c0:c1] if r is None else s16[:, c0:c1].rearrange(r[0], **r[1])))
                                      for (d, c0, c1, r) in dst_pieces],
                 reads=[t16], writes=[T_dst], dma_T=t16, npieces=len(dst_pieces))

        for i in range(4):
            li, si = i // 2, i % 2
            for g, wsrc in ((0, w_gate), (1, w_up)):
                for kc in range(8):
                    dst = WGU[i, :, :, g, kc, :].rearrange("fc p f -> p fc f")
                    prep_rows([(0, 128, wsrc[li, si, kc * 128:(kc + 1) * 128, :])], DFF, gall[:, i * 8 + kc:i * 8 + kc + 1],
                              [(dst, 0, DFF, ("p (fc f) -> p fc f", dict(f=128)))], dT("WGU"))
            for fc in range(NFC):
                pieces = [(WD[i, hf, :, fc, :], hf * 512, (hf + 1) * 512, None) for hf in range(2)]
                prep_rows([(0, 128, w_down[li, si, fc * 128:(fc + 1) * 128, :])], D, None, pieces, dT("WD"))
        for kc in range(8):
            rs = slice(kc * 128, (kc + 1) * 128)
            prep_rows([(0, 128, even_w_in[0, rs, :])], EVEN_IN, gall[:, 32 + kc:33 + kc],
                      [(EWIN[:, kc, :], 0, EVEN_IN, None)], dT("EWIN"))
            pieces = [(OWIN[:, kc, i * 64:(i + 1) * 64], QORDER[i] * 64, QORDER[i] * 64 + 64, None) for i in range(16)]
            pieces.append((OWIN[:, kc, 1024:1536], 1024, 1536, None))
            prep_rows([(0, 128, odd_w_in[0, rs, :])], 1536, gall[:, 40 + kc:41 + kc], pieces, dT("OWIN"))
            prep_rows([(0, 128, even_w_out[0, rs, :])], 1024, gall[:, 48 + kc:49 + kc] if kc < 4 else None,
                      [(EWOUT[:, kc, :], 0, 1024, None)], dT("EWOUT"))
            h0, h1 = QORDER[2 * kc], QORDER[2 * kc + 1]
            prep_rows([(0, 64, odd_w_out[0, h0 * 64:(h0 + 1) * 64, :]), (64, 128, odd_w_out[0, h1 * 64:(h1 + 1) * 64, :])],
                      1024, None, [(OWOUT[:, kc, :], 0, 1024, None)], dT("OWOUT"))
        for kc in range(2):
            prep_rows([(0, 128, mla_w_qb[0, kc * 128:(kc + 1) * 128, :])], 768, gall[:, 52 + kc:53 + kc],
                      [(WQB[:, kc, 0:768], 0, 768, None)], dT("WQB"))
        prep_rows([(0, 128, mla_w_kvb[0, :, :])], 1024, gall[:, 54:55], [(WQB[:, 2, :], 0, 1024, None)], dT("WQB"))
        P.barrier()
        P.op("sync", lambda e: [e.dma_start(out=wqb3[:, 0:2, 0:768], in_=WQB[:, 0:2, 0:768]),
                                e.dma_start(out=wqb3[:, 2, :], in_=WQB[:, 2, :])],
             reads=[dT("WQB")], writes=[T_wqb], dma_T=T_wqb, npieces=2)
        A.release(mC, P)

        TS = cfg.tile_sub
        NT = TS * 128
        cnt = {"rms": 0, "ev": 0, "h": 0}

        def tile_info(row0, rows_list):
            nsub = len(rows_list)
            ntok = 128 * (nsub - 1) + rows_list[-1]
            nb0 = 0 if row0 == 0 else 1 + (row0 - NMETA) // 128
            return nsub, ntok, nb0

        def tm_dram(dram2d, row0, rows_list):
            nsub = len(rows_list)
            if rows_list[0] < 128:
                return dram2d[row0:row0 + rows_list[0], :].unsqueeze(1)
            return dram2d[row0:row0 + 128 * nsub, :].rearrange("(s p) d -> p s d", p=128)

        def tm_sb(ap3, rows_list):
            nsub = len(rows_list)
            return ap3[:rows_list[0], :nsub, :]

        class Common:
            pass

        def alloc_common(n_h=2):
            C = Common()
            C.hbuf = [A.f32("h%d" % i, TS * D) for i in range(n_h)]
            C.hnT, C.T_hnT = A.bf16("hnT", 8 * NT)
            C.hnT3 = C.hnT.rearrange("p (k t) -> p k t", t=NT)
            C.ssq, C.T_ssq = A.f32("ssq", 8)
            C.junk, C.T_junk = A.f32("junk", D)
            C.hn16 = [A.bf16("hn16_%d" % i, D) for i in range(2)]
            return C

        def load_h(C, src, seq, row0, rows_list):
            a, t = C.hbuf[cnt["h"] % len(C.hbuf)]
            cnt["h"] += 1
            h3 = a.rearrange("p (s d) -> p s d", d=D)
            if isinstance(src, str):
                if row0 == 0:
                    LD(h3[:NMETA, 0, :], meta_in, t)
                else:
                    LD(tm_sb(h3, rows_list), tm_dram(x_in[seq], row0 - NMETA, rows_list), t)
            else:
                LD(tm_sb(h3, rows_list), tm_dram(src[seq], row0, rows_list), t, reads=[dT(src.name + str(seq))])
            return h3, t

        def store_h(dst, seq, row0, rows_list, h3, t):
            if isinstance(dst, str):
                if row0 == 0:
                    return
                ST(tm_dram(y_out[seq], row0 - NMETA, rows_list), tm_sb(h3, rows_list), t, dT("y"))
            else:
                ST(tm_dram(dst[seq], row0, rows_list), tm_sb(h3, rows_list), t, dT(dst.name + str(seq)))

        def transpose_to_T(C, src_fn, rows_list, nk=8):
            for s, rows in enumerate(rows_list):
                bank = 6 + (cnt["rms"] % 2)
                cnt["rms"] += 1
                pT = pb16(bank)
                for kc in range(nk):
                    ap, t = src_fn(s, kc)
                    PE(lambda e, kc=kc, rows=rows, ap=ap, pT=pT: e.transpose(out=pT[:, kc, :rows], in_=ap,
                                                                             identity=ident[:rows, :rows]),
                       [t, T_ident], [Tpb[bank]])
                SC(lambda e, s=s, rows=rows, pT=pT: e.copy(out=C.hnT3[:, :nk, s * 128:s * 128 + rows], in_=pT[:, :nk, :rows]),
                   [Tpb[bank]], [C.T_hnT])

        def rms_to_T(C, h3, T_h, rows_list):
            nsub = len(rows_list)
            rows0 = rows_list[0]
            for s, rows in enumerate(rows_list):
                SC(lambda e, s=s, rows=rows: e.activation(out=C.junk[:rows, :], in_=h3[:rows, s, :], func=AF.Square,
                                                          scale=1.0 / 32.0, accum_out=C.ssq[:rows, s:s + 1]),
                   [T_h], [C.T_junk, C.T_ssq])
            rsqrt_chain(C.ssq[:rows0, :nsub], C.T_ssq, 1.0)
            for s, rows in enumerate(rows_list):
                a16, t16 = C.hn16[s % 2]
                V(lambda e, s=s, rows=rows, a16=a16: e.tensor_scalar(out=a16[:rows, :], in0=h3[:rows, s, :],
                                                                     scalar1=C.ssq[:rows, s:s + 1], scalar2=None, op0=ALU.mult),
                  [T_h, C.T_ssq], [t16])
                transpose_to_T_one(C, a16, t16, s, rows)

        def transpose_to_T_one(C, a16, t16, s, rows, nk=8):
            bank = 6 + (cnt["rms"] % 2)
            cnt["rms"] += 1
            pT = pb16(bank)
            for kc in range(nk):
                PE(lambda e, kc=kc: e.transpose(out=pT[:, kc, :rows], in_=a16[:rows, kc * 128:(kc + 1) * 128],
                                                identity=ident[:rows, :rows]), [t16, T_ident], [Tpb[bank]])
            SC(lambda e: e.copy(out=C.hnT3[:, :nk, s * 128:s * 128 + rows], in_=pT[:, :nk, :rows]), [Tpb[bank]], [C.T_hnT])

        def proj_residual(C, h3, T_h, rows_list, wres, T_wres, banks=(3, 4)):
            for s, rows in enumerate(rows_list):
                for hf in range(2):
                    bank = banks[hf]
                    for kc in range(8):
                        PE(lambda e, s=s, rows=rows, hf=hf, kc=kc, bank=bank: e.matmul(
                            pbank[bank][:rows, :], lhsT=C.hnT3[:, kc, s * 128:s * 128 + rows],
                            rhs=wres[:, kc, hf * 512:(hf + 1) * 512], start=(kc == 0), stop=(kc == 7)),
                           [C.T_hnT, T_wres], [Tpb[bank]])
                    V(lambda e, s=s, rows=rows, hf=hf, bank=bank: e.tensor_tensor(
                        out=h3[:rows, s, hf * 512:(hf + 1) * 512], in0=pbank[bank][:rows, :],
                        in1=h3[:rows, s, hf * 512:(hf + 1) * 512], op=ALU.add), [Tpb[bank], T_h], [T_h])

        def phase_F(seq, src, dst, ffn_ids):
            m = A.mark()
            C = alloc_common(2)
            act = [A.bf16("act%d" % f, NT) for f in range(NFC)]
            wgu_slots = []
            for i in range(4):
                a, t = A.bf16("wgu%d" % i, 2 * 8 * 128)
                wgu_slots.append((a.rearrange("p (g k f) -> p g k f", g=2, k=8), t))
            wd_slots = []
            for i in range(2):
                a, t = A.bf16("wd%d" % i, NFC * 512)
                wd_slots.append((a.rearrange("p (fc c) -> p fc c", c=512), t))
            sg = [A.f32("sg%d" % i, NT) for i in range(2)]
            wgu_ring = Ring(P, wgu_slots)
            wd_ring = Ring(P, wd_slots)
            for _ in cfg.tiles:
                for i in ffn_ids:
                    wgu_ring.extend([(lambda ap, i=i, fc=fc: [(ap, WGU[i, fc])]) for fc in range(NFC)])
                    wd_ring.extend([(lambda ap, i=i, hf=hf: [(ap, WD[i, hf])]) for hf in range(2)])

            def ffn(h3, T_h, rows_list):
                nsub, ntok, _ = tile_info(0, rows_list)
                for fc in range(NFC):
                    w, tw = wgu_ring.next()
                    k = fc % 2
                    for g in range(2):
                        bank = 2 * g + k
                        for kc in range(8):
                            PE(lambda e, g=g, kc=kc, bank=bank, w=w: e.matmul(
                                pbank[bank][:, :ntok], lhsT=w[:, g, kc, :], rhs=C.hnT3[:, kc, :ntok],
                                start=(kc == 0), stop=(kc == 7)), [tw, C.T_hnT], [Tpb[bank]])
                    wgu_ring.release()
                    sga, tsg = sg[k]
                    SC(lambda e, k=k, sga=sga: e.activation(out=sga[:, :ntok], in_=pbank[k][:, :ntok], func=AF.Silu),
                       [Tpb[k]], [tsg])
                    aa, ta = act[fc]
                    V(lambda e, k=k, sga=sga, aa=aa: e.tensor_tensor(out=aa[:, :ntok], in0=sga[:, :ntok],
                                                                     in1=pbank[2 + k][:, :ntok], op=ALU.mult),
                      [tsg, Tpb[2 + k]], [ta])
                for hf in range(2):
                    wd, twd = wd_ring.next()
                    for s, rows in enumerate(rows_list):
                        bank = 4 + (cnt["ev"] % 2)
                        cnt["ev"] += 1
                        for fc in range(NFC):
                            aa, ta = act[fc]
                            PE(lambda e, fc=fc, s=s, rows=rows, bank=bank, aa=aa, wd=wd: e.matmul(
                                pbank[bank][:rows, :], lhsT=aa[:, s * 128:s * 128 + rows], rhs=wd[:, fc, :],
                                start=(fc == 0), stop=(fc == NFC - 1)), [ta, twd], [Tpb[bank]])
                        V(lambda e, s=s, rows=rows, bank=bank, hf=hf: e.scalar_tensor_tensor(
                            out=h3[:rows, s, hf * 512:(hf + 1) * 512], in0=pbank[bank][:rows, :], scalar=0.5,
                            in1=h3[:rows, s, hf * 512:(hf + 1) * 512], op0=ALU.mult, op1=ALU.add),
                          [Tpb[bank], T_h], [T_h])
                    wd_ring.release()

            for (row0, rows_list) in cfg.tiles:
                h3, th = load_h(C, src, seq, row0, rows_list)
                for _ in ffn_ids:
                    rms_to_T(C, h3, th, rows_list)
                    ffn(h3, th, rows_list)
                store_h(dst, seq, row0, rows_list, h3, th)
            P.barrier()
            A.release(m, P)

        def phase_E(seq):
            m = A.mark()
            C = alloc_common(1)
            ewin, T_ewin = A.bf16("ewin", 8 * EVEN_IN)
            ewin3 = ewin.rearrange("p (k n) -> p k n", k=8)
            LD(ewin3, EWIN, T_ewin, reads=[dT("EWIN")])
            ropet = [A.f32("rope%d" % i, TS * NROPE) for i in range(2)]
            stQK, T_stQK = A.bf16("stQK", 8 * NT)
            stQK3 = stQK.rearrange("p (a t) -> p a t", a=8)
            v16st, T_v16 = A.bf16("v16st", TS * 512)
            v16st3 = v16st.rearrange("p (s d) -> p s d", s=TS)
            gast, T_gast = A.f32("gast", TS * 512)
            gast3 = gast.rearrange("p (s d) -> p s d", s=TS)
            kvst, T_kvst = A.f32("kvst", TS * 512)
            kvst3 = kvst.rearrange("p (s x) -> p s x", s=TS)
            stMQ, T_stMQ = A.bf16("stMQ", 8 * NT)
            stMQ3 = stMQ.rearrange("p (h t) -> p h t", h=8)
            stMK, T_stMK = A.bf16("stMK", 8 * NT)
            stMK3 = stMK.rearrange("p (h t) -> p h t", h=8)
            vxst, T_vxst = A.bf16("vxst", TS * 8 * VS)
            vxst4 = vxst.rearrange("p (s h e) -> p s h e", s=TS, h=8)
            G(lambda e: e.memset(vxst, 1.0), [], [T_vxst])
            zqk, T_zqk = A.f32("zqk", 1024)
            tA, T_tA = A.f32("tA", 512)
            tB, T_tB = A.f32("tB", 512)
            tC, T_tC = A.f32("tC", 512)
            tD, T_tD = A.f32("tD", 512)
            qk16, T_qk16 = A.bf16("qk16", 1024)
            vw, T_vw = A.bf16("vw", 1024)
            c32, T_c32 = A.f32("c32", 416)
            ssm, T_ssm = A.f32("ssm", 2)
            cn16, T_cn16 = A.bf16("cn16", 384)
            cT, T_cT = A.bf16("cT", 3 * 128)
            cT3 = cT.rearrange("p (k t) -> p k t", k=3)
            qkm, T_qkm = A.f32("qkm", 2 * 768)
            sq, T_sq = A.f32("sq", 2 * 768)
            ssh, T_ssh = A.f32("ssh", 16)
            qkf, T_qkf = A.bf16("qkf", 2 * 768)
            rT = [T("rope_t%d" % i) for i in range(2)]

            for ti, (row0, rows_list) in enumerate(cfg.tiles):
                nsub, ntok, nb0 = tile_info(row0, rows_list)
                h3, th = load_h(C, H1, seq, row0, rows_list)
                ra, rt = ropet[ti % 2]
                rope3 = ra.rearrange("p (s c) -> p s c", s=TS)
                LD(tm_sb(rope3, rows_list), tm_dram(ROPE, row0, rows_list), rt)
                rms_to_T(C, h3, th, rows_list)
                for s, rows in enumerate(rows_list):
                    tok = slice(s * 128, s * 128 + rows)
                    wkt, T_wkt = (wk3, T_wk) if rows == 128 else (wkM3, T_wkM)
                    for nch in range(5):
                        ncols = 512 if nch < 4 else EVEN_IN - 2048
                        for kc in range(8):
                            PE(lambda e, nch=nch, ncols=ncols, kc=kc: e.matmul(
                                pbank[nch][:rows, :ncols], lhsT=C.hnT3[:, kc, tok],
                                rhs=ewin3[:, kc, nch * 512:nch * 512 + ncols], start=(kc == 0), stop=(kc == 7)),
                               [C.T_hnT, T_ewin], [Tpb[nch]])
                    SC(lambda e: e.copy(out=zqk[:rows, 0:512], in_=pbank[0][:rows, :]), [Tpb[0]], [T_zqk])
                    SC(lambda e: e.copy(out=zqk[:rows, 512:1024], in_=pbank[1][:rows, :]), [Tpb[1]], [T_zqk])
                    SC(lambda e: e.copy(out=v16st3[:rows, s, :], in_=pbank[2][:rows, :]), [Tpb[2]], [T_v16])
                    SC(lambda e: e.activation(out=gast3[:rows, s, :], in_=pbank[3][:rows, :], func=AF.Silu), [Tpb[3]], [T_gast])
                    SC(lambda e: e.copy(out=c32[:rows, :], in_=pbank[4][:rows, :416]), [Tpb[4]], [T_c32])
                    if DBG < 1:
                        continue
                    z5 = zqk[:rows, :].rearrange("p (a h t d) -> p a h t d", a=2, h=8, t=2)
                    x1 = z5[:, :, :, 0, :]
                    x2 = z5[:, :, :, 1, :]
                    r4 = rope3[:rows, s, 0:128].rearrange("p (a t d) -> p a t d", a=2, t=2)
                    cosb = r4[:, :, 0, :].unsqueeze(2).to_broadcast([rows, 2, 8, 32])
                    sinb = r4[:, :, 1, :].unsqueeze(2).to_broadcast([rows, 2, 8, 32])
                    o5 = qk16[:rows, :].rearrange("p (a h t d) -> p a h t d", a=2, h=8, t=2)

                    def v4(ap):
                        return ap[:rows, :].rearrange("p (a h d) -> p a h d", a=2, h=8)
                    V(lambda e: e.tensor_tensor(out=v4(tA), in0=x1, in1=cosb, op=ALU.mult), [T_zqk, rt], [T_tA])
                    V(lambda e: e.tensor_tensor(out=v4(tB), in0=x2, in1=sinb, op=ALU.mult), [T_zqk, rt], [T_tB])
                    V(lambda e: e.tensor_tensor(out=o5[:, :, :, 0, :], in0=v4(tA), in1=v4(tB), op=ALU.subtract),
                      [T_tA, T_tB], [T_qk16])
                    G(lambda e: e.tensor_tensor(out=v4(tC), in0=x2, in1=cosb, op=ALU.mult), [T_zqk, rt], [T_tC])
                    G(lambda e: e.tensor_tensor(out=v4(tD), in0=x1, in1=sinb, op=ALU.mult), [T_zqk, rt], [T_tD])
                    G(lambda e: e.tensor_tensor(out=o5[:, :, :, 1, :], in0=v4(tC), in1=v4(tD), op=ALU.add),
                      [T_tC, T_tD], [T_qk16])
                    if DBG < 2:
                        continue
                    pT = pb16(5)
                    for a in range(8):
                        PE(lambda e, a=a: e.transpose(out=pT[:, a, :rows], in_=qk16[:rows, a * 128:(a + 1) * 128],
                                                      identity=ident[:rows, :rows]), [T_qk16, T_ident], [Tpb[5]])
                    SC(lambda e: e.copy(out=stQK3[:, :, tok], in_=pT[:, :, :rows]), [Tpb[5]], [T_stQK])
                    if DBG < 3:
                        continue
                    vw5 = vw[:rows, :].rearrange("p (c f h d) -> p c f h d", c=4, f=2, h=2)
                    vin = v16st3[:rows, s, :].rearrange("p (c h d) -> p c h d", c=4, h=2)
                    for f in range(2):
                        wb = wkt[:rows, f, :].rearrange("p (c h) -> p c h", h=2).unsqueeze(3).to_broadcast([rows, 4, 2, 64])
                        G(lambda e, f=f, wb=wb: e.tensor_tensor(out=vw5[:, :, f, :, :], in0=vin, in1=wb, op=ALU.mult),
                          [T_v16, T_wkt], [T_vw])
                    for c in range(4):
                        bank = 5 + c // 2 if False else (3 if c < 2 else 4)
                        PE(lambda e, c=c, bank=bank: e.matmul(pbank[bank][:, (c % 2) * 256:(c % 2) * 256 + 256],
                                                              lhsT=qk16[:rows, 512 + c * 128:512 + (c + 1) * 128],
                                                              rhs=vw[:rows, c * 256:(c + 1) * 256], start=True, stop=True),
                           [T_qk16, T_vw], [Tpb[bank]])
                    kv5 = kvst3[:, s, :].rearrange("p (c f e) -> p c f e", c=4, f=2)
                    for b in range(2):
                        pv = pbank[3 + b].rearrange("p (c f h e) -> p c f h e", c=2, f=2, h=2)
                        V(lambda e, b=b, pv=pv: e.tensor_copy(out=kv5[0:64, 2 * b:2 * b + 2, :, :], in_=pv[0:64, :, :, 0, :]),
                          [Tpb[3 + b]], [T_kvst])
                        SC(lambda e, b=b, pv=pv: e.copy(out=kv5[64:128, 2 * b:2 * b + 2, :, :], in_=pv[64:128, :, :, 1, :]),
                           [Tpb[3 + b]], [T_kvst])
                    if DBG < 4:
                        continue
                    SC(lambda e: e.activation(out=C.junk[:rows, :256], in_=c32[:rows, 0:256], func=AF.Square, scale=1.0 / 16.0,
                                              accum_out=ssm[:rows, 0:1]), [T_c32], [C.T_junk, T_ssm])
                    SC(lambda e: e.activation(out=C.junk[:rows, :128], in_=c32[:rows, 256:384], func=AF.Square,
                                              scale=float(128 ** -0.5), accum_out=ssm[:rows, 1:2]), [T_c32], [C.T_junk, T_ssm])
                    rsqrt_chain(ssm[:rows, :], T_ssm, 1.0)
                    V(lambda e: e.tensor_scalar(out=cn16[:rows, 0:256], in0=c32[:rows, 0:256], scalar1=ssm[:rows, 0:1],
                                                scalar2=None, op0=ALU.mult), [T_c32, T_ssm], [T_cn16])
                    V(lambda e: e.tensor_scalar(out=cn16[:rows, 256:384], in0=c32[:rows, 256:384], scalar1=ssm[:rows, 1:2],
                                                scalar2=None, op0=ALU.mult), [T_c32, T_ssm], [T_cn16])
                    if DBG < 4.2:
                        continue
                    pC = pb16(6)
                    for j in range(3):
                        PE(lambda e, j=j: e.transpose(out=pC[:, j, :rows], in_=cn16[:rows, j * 128:(j + 1) * 128],
                                                      identity=ident[:rows, :rows]), [T_cn16, T_ident], [Tpb[6]])
                    SC(lambda e: e.copy(out=cT3[:, :, :rows], in_=pC[:, 0:3, :rows]), [Tpb[6]], [T_cT])
                    if DBG < 4.4:
                        continue
                    for (bank, c0, c1) in ((7, 0, 480), (0, 480, 768)):
                        for kc in range(2):
                            PE(lambda e, bank=bank, c0=c0, c1=c1, kc=kc: e.matmul(
                                pbank[bank][:rows, :c1 - c0], lhsT=cT3[:, kc, :rows], rhs=wqb3[:, kc, c0:c1],
                                start=(kc == 0), stop=(kc == 1)), [T_cT, T_wqb], [Tpb[bank]])
                    if DBG < 4.6:
                        continue
                    for (bank, c0) in ((1, 0), (2, 512)):
                        PE(lambda e, bank=bank, c0=c0: e.matmul(pbank[bank][:rows, :], lhsT=cT3[:, 2, :rows],
                                                                rhs=(wqb3[:, 1, 0:512] if DBG == 4.7 else (wqb3[:, 2, 0:512] if DBG == 4.8 else wkvb[:, c0:c0 + 512])), start=True, stop=True),
                           [T_cT, T_wkvb], [Tpb[bank]])
                    if DBG < 5:
                        continue
                    qk4 = qkm[:rows, :].rearrange("p (a h d) -> p a h d", a=2, h=8)
                    SC(lambda e: e.copy(out=qkm[:rows, 0:480], in_=pbank[7][:rows, :480]), [Tpb[7]], [T_qkm])
                    SC(lambda e: e.copy(out=qkm[:rows, 480:768], in_=pbank[0][:rows, :288]), [Tpb[0]], [T_qkm])
                    for b in range(2):
                        pv = pbank[1 + b][:rows, :].rearrange("p (h t d) -> p h t d", h=4, t=2)
                        if DBG >= 5.2:
                            V(lambda e, b=b, pv=pv: e.tensor_copy(out=qk4[:, 1, 4 * b:4 * b + 4, 0:64], in_=pv[:, :, 0, :]),
                              [Tpb[1 + b]], [T_qkm])
                        if DBG >= 5.3:
                            V(lambda e, b=b, pv=pv: e.tensor_copy(out=vxst4[:rows, s, 4 * b:4 * b + 4, 0:64], in_=pv[:, :, 1, :]),
                              [Tpb[1 + b]], [T_vxst])
                    if DBG < 5.4:
                        continue
                    G(lambda e: e.tensor_copy(out=qk4[:, 1, :, 64:96],
                                              in_=c32[:rows, 384:416].unsqueeze(1).to_broadcast([rows, 8, 32])),
                      [T_c32], [T_qkm])
                    if DBG < 6:
                        continue
                    G(lambda e: e.tensor_tensor(out=sq[:rows, :], in0=qkm[:rows, :], in1=qkm[:rows, :], op=ALU.mult),
                      [T_qkm], [T_sq])
                    V(lambda e: e.tensor_reduce(out=ssh[:rows, :], in_=sq[:rows, :].rearrange("p (g d) -> p g d", d=96),
                                                axis=AX.X, op=ALU.add), [T_sq], [T_ssh])
                    rsqrt_chain(ssh[:rows, :], T_ssh, 1.0 / 96.0)
                    q3 = qkm[:rows, :].rearrange("p (g d) -> p g d", d=96)
                    V(lambda e: e.tensor_tensor(out=q3, in0=q3, in1=ssh[:rows, :].unsqueeze(2).to_broadcast([rows, 16, 96]),
                                                op=ALU.mult), [T_qkm, T_ssh], [T_qkm])
                    G(lambda e: e.tensor_tensor(out=qk4, in0=qk4, in1=gqk3[:rows, :, :].unsqueeze(2).to_broadcast([rows, 2, 8, 96]),
                                                op=ALU.mult), [T_qkm, T_gqk], [T_qkm])
                    if DBG < 7:
                        continue
                    f3 = qkf[:rows, :].rearrange("p (g d) -> p g d", d=96)
                    xa = q3[:, :, 64:80]
                    xb = q3[:, :, 80:96]
                    cB = rope3[:rows, s, 128:144].unsqueeze(1).to_broadcast([rows, 16, 16])
                    sB = rope3[:rows, s, 144:160].unsqueeze(1).to_broadcast([rows, 16, 16])

                    def v3(ap):
                        return ap[:rows, 0:256].rearrange("p (g d) -> p g d", d=16)
                    V(lambda e: e.tensor_tensor(out=v3(tA), in0=xa, in1=cB, op=ALU.mult), [T_qkm, rt], [T_tA])
                    V(lambda e: e.tensor_tensor(out=v3(tB), in0=xb, in1=sB, op=ALU.mult), [T_qkm, rt], [T_tB])
                    V(lambda e: e.tensor_tensor(out=f3[:, :, 64:80], in0=v3(tA), in1=v3(tB), op=ALU.subtract),
                      [T_tA, T_tB], [T_qkf])
                    G(lambda e: e.tensor_tensor(out=v3(tC), in0=xb, in1=cB, op=ALU.mult), [T_qkm, rt], [T_tC])
                    G(lambda e: e.tensor_tensor(out=v3(tD), in0=xa, in1=sB, op=ALU.mult), [T_qkm, rt], [T_tD])
                    G(lambda e: e.tensor_tensor(out=f3[:, :, 80:96], in0=v3(tC), in1=v3(tD), op=ALU.add),
                      [T_tC, T_tD], [T_qkf])
                    V(lambda e: e.tensor_copy(out=f3[:, :, 0:64], in_=q3[:, :, 0:64]), [T_qkm], [T_qkf])
                    if DBG < 8:
                        continue
                    for a, (bank, stg, tstg) in enumerate(((3, stMQ3, T_stMQ), (4, stMK3, T_stMK))):
                        pM = pb16(bank)
                        for h in range(8):
                            PE(lambda e, a=a, h=h, pM=pM: e.transpose(out=pM[:96, h, :rows], in_=f3[:, a * 8 + h, :],
                                                                      identity=ident[:rows, :rows]),
                               [T_qkf, T_ident], [Tpb[bank]])
                        V(lambda e, pM=pM, stg=stg: e.tensor_copy(out=stg[:96, :, tok], in_=pM[:96, :, :rows]),
                          [Tpb[bank]], [tstg])
                if DBG < 9:
                    continue
                r0 = row0
                sq_ = str(seq)
                P.op("gpsimd", lambda e: [e.dma_start(out=RQT[seq, :, :, r0:r0 + ntok].rearrange("c p t -> p c t"), in_=stQK3[:, 0:4, :ntok]),
                                          e.dma_start(out=RKT[seq, :, :, r0:r0 + ntok].rearrange("c p t -> p c t"), in_=stQK3[:, 4:8, :ntok])],
                     reads=[T_stQK], writes=[dT("RQT" + sq_), dT("RKT" + sq_)], dma_T=T_stQK, npieces=2)
                ST(tm_dram(RV[seq], r0, rows_list), tm_sb(v16st3, rows_list), T_v16, dT("RV" + sq_))
                ST(tm_dram(RG[seq], r0, rows_list), tm_sb(gast3, rows_list), T_gast, dT("RG" + sq_))
                ST(KVS[seq, nb0:nb0 + nsub].rearrange("s p x -> p s x"), kvst3[:, :nsub, :], T_kvst, dT("KVS" + sq_))
                ST(MQT[seq, :, :, r0:r0 + ntok].rearrange("h p t -> p h t"), stMQ3[:96, :, :ntok], T_stMQ, dT("MQT" + sq_))
                ST(MKT[seq, :, :, r0:r0 + ntok].rearrange("h p t -> p h t"), stMK3[:96, :, :ntok], T_stMK, dT("MKT" + sq_))
                rws = rows_list[0]
                ST(MV[seq, 0:rws, nb0:nb0 + nsub, :, :], vxst4[:rws, :nsub, :, :], T_vxst, dT("MV" + sq_))
            P.barrier()
            A.release(m, P)

        def phase_scan(seq):
            m = A.mark()
            sq_ = str(seq)
            kvall, T_kvall = A.f32("kvall", NBLK * 512)
            kv5 = kvall.rearrange("p (n c f e) -> p n c f e", n=NBLK, c=4, f=2)
            LD(kvall.rearrange("p (n x) -> p n x", n=NBLK), KVS[seq].rearrange("n p x -> p n x"), T_kvall, reads=[dT("KVS" + sq_)])
            outs = []
            for f, (eng, order) in enumerate((("vector", list(range(NBLK))), ("gpsimd", list(range(NBLK - 1, -1, -1))))):
                so, T_so = A.bf16("so%d" % f, NBLK * 256)
                so4 = so.rearrange("p (n c e) -> p n c e", n=NBLK, c=4)
                stt, T_stt = A.f32("stt%d" % f, 256)
                st3 = stt.rearrange("p (c e) -> p c e", c=4)
                P.op(eng, lambda e, stt=stt: e.memset(stt, 0.0), [], [T_stt])
                for nb in order:
                    P.op(eng, lambda e, nb=nb, so4=so4, st3=st3: e.tensor_copy(out=so4[:, nb, :, :], in_=st3), [T_stt], [T_so])
                    P.op(eng, lambda e, st3=st3, f=f: e.tensor_tensor(out=st3, in0=st3, in1=cfb4[:, :, f, :], op=ALU.mult),
                         [T_stt, T_cfb], [T_stt])
                    P.op(eng, lambda e, nb=nb, st3=st3, f=f: e.tensor_tensor(out=st3, in0=st3, in1=kv5[:, nb, :, f, :], op=ALU.add),
                         [T_stt, T_kvall], [T_stt])
                outs.append((so, T_so))
            for f, (so, T_so) in enumerate(outs):
                dst = SFF if f == 0 else SFB
                ST(dst[seq].rearrange("n p x -> p n x"), so.rearrange("p (n x) -> p n x", n=NBLK), T_so,
                   dT(("SFF" if f == 0 else "SFB") + sq_))
            P.barrier()
            A.release(m, P)

        def phase_M(seq):
            m = A.mark()
            sq_ = str(seq)
            kT, T_kT = A.bf16("kT", 8 * L)
            kT3 = kT.rearrange("p (h t) -> p h t", h=8)
            vx, T_vx = A.bf16("vx", 8 * NBLK * VS)
            vx4 = vx.rearrange("p (n h e) -> p n h e", h=8, n=NBLK)
            LD(kT3[:96, :, :], MKT[seq].rearrange("h p t -> p h t"), T_kT, reads=[dT("MKT" + sq_)])
            P.op("sync", lambda e: [e.dma_start(out=vx4[:, 1:, :, :], in_=MV[seq, :, 1:, :, :]),
                                    e.dma_start(out=vx4[:NMETA, 0:1, :, :], in_=MV[seq, 0:NMETA, 0:1, :, :])],
                 reads=[dT("MV" + sq_)], writes=[T_vx], dma_T=T_vx, npieces=2)
            qs = []
            for i in range(2):
                a, t = A.bf16("qT%d" % i, 8 * NT)
                qs.append((a.rearrange("p (h t) -> p h t", h=8), t))
            pts = [A.bf16("pt%d" % i, NT) for i in range(4)]
            ost, T_ost = A.bf16("ost", TS * 512)
            ost3 = ost.rearrange("p (s d) -> p s d", s=TS)
            rc, T_rc = A.f32("rc", 8)
            oTs = [A.f32("oT%d" % i, NT) for i in range(2)]
            tiles = cfg.tiles
            qdone = set()

            def ensure_q(ti):
                if ti in qdone or ti >= len(tiles):
                    return
                qdone.add(ti)
                row0, rows_list = tiles[ti]
                nsub, ntok, nb0 = tile_info(row0, rows_list)
                q3, tq = qs[ti % 2]
                LD(q3[:96, :, :ntok], MQT[seq, :, :, row0:row0 + ntok].rearrange("h p t -> p h t"), tq, reads=[dT("MQT" + sq_)])

            steps = [(ti, h, nb) for ti in range(len(tiles)) for h in range(8) for nb in range(NBLK)]

            def info(i):
                ti, h, nb = steps[i]
                row0, rows_list = tiles[ti]
                nsub, ntok, nb0 = tile_info(row0, rows_list)
                krows = NMETA if nb == 0 else 128
                kc0 = 0 if nb == 0 else NMETA + 128 * (nb - 1)
                q3, tq = qs[ti % 2]
                return ti, h, nb, rows_list, nsub, ntok, krows, kc0, q3, tq, (0, 1, 6)[i % 3], pts[i % 4], 2 + (h % 2)

            def ST_(i):
                ti, h, nb, rows_list, nsub, ntok, krows, kc0, q3, tq, sb_, (pt, tpt), accb = info(i)
                ensure_q(ti)
                PE(lambda e: e.matmul(pbank[sb_][:krows, :ntok], lhsT=kT3[:96, h, kc0:kc0 + krows], rhs=q3[:96, h, :ntok],
                                      start=True, stop=True), [T_kT, tq], [Tpb[sb_]])

            def EXP_PV(i):
                ti, h, nb, rows_list, nsub, ntok, krows, kc0, q3, tq, sb_, (pt, tpt), accb = info(i)
                SC(lambda e: e.activation(out=pt[:krows, :ntok], in_=pbank[sb_][:krows, :ntok], func=AF.Exp), [Tpb[sb_]], [tpt])
                PE(lambda e: e.matmul(pbank[accb][:65, :ntok], lhsT=vx4[:krows, nb, h, 0:65], rhs=pt[:krows, :ntok],
                                      start=(nb == 0), stop=(nb == NBLK - 1)), [tpt, T_vx], [Tpb[accb]])
                if nb == NBLK - 1:
                    rows0 = rows_list[0]
                    oa, toa = oTs[h % 2]
                    tb = 4 + (h % 2)
                    V(lambda e: e.tensor_copy(out=oa[:65, :ntok], in_=pbank[accb][:65, :ntok]), [Tpb[accb]], [toa])
                    for s, rows in enumerate(rows_list):
                        PE(lambda e, s=s, rows=rows: e.transpose(out=pbank[tb][:rows, s * 65:(s + 1) * 65],
                                                                 in_=oa[:65, s * 128:s * 128 + rows], identity=identf[:65, :65]),
                           [toa, T_identf], [Tpb[tb]])
                    accb = tb
                    acc3 = pbank[accb][:rows0, :nsub * 65].rearrange("p (s e) -> p s e", e=65)
                    V(lambda e: e.reciprocal(out=rc[:rows0, :nsub], in_=acc3[:, :, 64]), [Tpb[accb]], [T_rc])
                    V(lambda e: e.tensor_tensor(
                        out=ost3[:rows0, :nsub, h * 64:(h + 1) * 64], in0=acc3[:, :, 0:64],
                        in1=rc[:rows0, :nsub].unsqueeze(2).to_broadcast([rows0, nsub, 64]), op=ALU.mult),
                      [Tpb[accb], T_rc], [T_ost])
                    if h == 7:
                        row0 = tiles[ti][0]
                        ST(tm_dram(MIXM[seq], row0, rows_list), tm_sb(ost3, rows_list), T_ost, dT("MIXM" + sq_))

            ST_(0)
            ST_(1)
            for i in range(len(steps)):
                ti, h, nb = steps[i]
                if h == 0 and nb == 0:
                    ensure_q(ti + 1)
                if i + 2 < len(steps):
                    ST_(i + 2)
                EXP_PV(i)
            P.barrier()
            A.release(m, P)

        def phase_R(seq, dst):
            m = A.mark()
            sq_ = str(seq)
            C = alloc_common(2)
            ewout, T_ewout = A.bf16("ewout", 8 * 1024)
            ewout3 = ewout.rearrange("p (k n) -> p k n", k=8)
            LD(ewout3, EWOUT, T_ewout, reads=[dT("EWOUT")])
            bufs = []
            for i in range(2):
                qk, tqk = A.bf16("rqk%d" % i, 8 * NT)
                v, tv = A.bf16("rv%d" % i, TS * 512)
                ga, tga = A.f32("rga%d" % i, TS * 512)
                sf, tsf = A.bf16("rsf%d" % i, 2 * TS * 4 * 128)
                mm, tmm = A.bf16("rmm%d" % i, TS * 512)
                qz, tqz = A.bf16("rqz%d" % i, 2 * 4 * NT)
                G(lambda e, sf=sf: e.memset(sf, 0.0), [], [tsf])
                G(lambda e, qz=qz: e.memset(qz, 0.0), [], [tqz])
                bufs.append((qk.rearrange("p (a t) -> p a t", a=8), tqk, v.rearrange("p (s d) -> p s d", s=TS), tv,
                             ga.rearrange("p (s d) -> p s d", s=TS), tga,
                             sf.rearrange("p (f s c x) -> p f s c x", f=2, s=TS, c=4), tsf,
                             mm.rearrange("p (s d) -> p s d", s=TS), tmm,
                             qz.rearrange("p (m c t) -> p m c t", m=2, c=4), tqz))
            mixR, T_mixR = A.bf16("mixR", TS * 512)
            mixR3 = mixR.rearrange("p (s d) -> p s d", s=TS)
            sdt, T_sdt = A.bf16("sdt", 8 * 128)
            sdt3 = sdt.rearrange("p (h i) -> p h i", h=8)
            qfb, T_qfb = A.bf16("qfb", 2 * 4 * 128)
            qfb4 = qfb.rearrange("p (f c i) -> p f c i", f=2, c=4)
            sqr, T_sqr = A.f32("sqr", 512)
            ssr, T_ssr = A.f32("ssr", 8)
            tmp32, T_tmp32 = A.f32("tmp32", 512)
            for ti, (row0, rows_list) in enumerate(cfg.tiles):
                nsub, ntok, nb0 = tile_info(row0, rows_list)
                rows0 = rows_list[0]
                h3, th = load_h(C, H1, seq, row0, rows_list)
                qk3, tqk, v3_, tv, ga3, tga, sf5, tsf, mm3, tmm, qz4, tqz = bufs[ti % 2]
                P.op("sync", lambda e: [e.dma_start(out=qz4[0:64, 0, :, :ntok], in_=RQT[seq, :, 0:64, row0:row0 + ntok].rearrange("c p t -> p c t")),
                                        e.dma_start(out=qz4[64:128, 1, :, :ntok], in_=RQT[seq, :, 64:128, row0:row0 + ntok].rearrange("c p t -> p c t"))],
                     reads=[dT("RQT" + sq_)], writes=[tqz], dma_T=tqz, npieces=2)
                P.op("sync", lambda e: [e.dma_start(out=qk3[:, 0:4, :ntok], in_=RQT[seq, :, :, row0:row0 + ntok].rearrange("c p t -> p c t")),
                                        e.dma_start(out=qk3[:, 4:8, :ntok], in_=RKT[seq, :, :, row0:row0 + ntok].rearrange("c p t -> p c t"))],
                     reads=[dT("RQT" + sq_), dT("RKT" + sq_)], writes=[tqk], dma_T=tqk, npieces=2)
                LD(tm_sb(v3_, rows_list), tm_dram(RV[seq], row0, rows_list), tv, reads=[dT("RV" + sq_)])
                LD(tm_sb(ga3, rows_list), tm_dram(RG[seq], row0, rows_list), tga, reads=[dT("RG" + sq_)])
                sfp = []
                for f_, src_ in enumerate((SFF, SFB)):
                    for hb_ in range(2):
                        for s_ in range(nsub):
                            sfp.append((sf5[64 * hb_:64 * hb_ + 64, f_, s_, :, 64 * hb_:64 * hb_ + 64],
                                        src_[seq, nb0 + s_, 64 * hb_:64 * hb_ + 64, :].rearrange("p (c e) -> p c e", c=4)))
                P.op("sync", lambda e: [e.dma_start(out=o_, in_=i_) for (o_, i_) in sfp],
                     reads=[dT("SFF" + sq_), dT("SFB" + sq_)], writes=[tsf], dma_T=tsf, npieces=len(sfp))
                LD(tm_sb(mm3, rows_list), tm_dram(MIXM[seq], row0, rows_list), tmm, reads=[dT("MIXM" + sq_)])
                for s, rows in enumerate(rows_list):
                    tok = slice(s * 128, s * 128 + rows)
                    boff = 0 if rows == 128 else 112
                    for h in range(8):
                        c, b0 = h // 2, 64 * (h % 2)
                        bank = h // 4
                        PE(lambda e, h=h, c=c, b0=b0, bank=bank: e.matmul(
                            pbank[bank][:rows, (h % 4) * 128:(h % 4) * 128 + rows], lhsT=qk3[:, 4 + c, tok],
                            rhs=qz4[:, h % 2, c, tok], start=True, stop=True), [tqk, tqz], [Tpb[bank]])
                    for b in range(2):
                        V(lambda e, b=b: e.tensor_tensor(
                            out=sdt3[:rows, 4 * b:4 * b + 4, :rows],
                            in0=pbank[b][:rows, :].rearrange("p (h i) -> p h i", i=128)[:, :, :rows],
                            in1=DT3[:rows, 4 * b:4 * b + 4, :rows], op=ALU.mult), [Tpb[b], T_DT], [T_sdt])
                    if DBG < 21:
                        continue
                    for f in range(2):
                        G(lambda e, f=f: e.tensor_tensor(out=qfb4[:, f, :, :rows], in0=qk3[:, 0:4, tok],
                                                         in1=qdec4[:, :, f, boff:boff + rows], op=ALU.mult),
                          [tqk, T_qdec], [T_qfb])
                    if DBG < 22:
                        continue
                    for h in range(8):
                        c, b0 = h // 2, 64 * (h % 2)
                        o_ap = pbank[2][:rows, h * 64:(h + 1) * 64]
                        PE(lambda e, h=h, o_ap=o_ap: e.matmul(o_ap, lhsT=sdt3[:rows, h, :rows], rhs=v3_[:rows, s, h * 64:(h + 1) * 64],
                                                              start=(h == 0), stop=False, skip_group_check=True), [T_sdt, tv], [Tpb[2]])
                    for c in range(4):
                        for f in range(2):
                            PE(lambda e, f=f, c=c: e.matmul(
                                pbank[2][:rows, c * 128:(c + 1) * 128], lhsT=qfb4[:, f, c, :rows], rhs=sf5[:, f, s, c, :],
                                start=False, stop=(f == 1), skip_group_check=True), [T_qfb, tsf], [Tpb[2]])
                    if DBG < 23:
                        continue
                    SC(lambda e: e.activation(out=sqr[:rows, :], in_=pbank[2][:rows, :], func=AF.Square), [Tpb[2]], [T_sqr])
                    V(lambda e: e.tensor_reduce(out=ssr[:rows, :], in_=sqr[:rows, :].rearrange("p (h d) -> p h d", d=64),
                                                axis=AX.X, op=ALU.add), [T_sqr], [T_ssr])
                    rsqrt_chain(ssr[:rows, :], T_ssr, 1.0 / 64.0)
                    V(lambda e: e.tensor_tensor(out=tmp32[:rows, :].rearrange("p (h d) -> p h d", d=64),
                                                in0=pbank[2][:rows, :].rearrange("p (h d) -> p h d", d=64),
                                                in1=ssr[:rows, :].unsqueeze(2).to_broadcast([rows, 8, 64]), op=ALU.mult),
                      [Tpb[2], T_ssr], [T_tmp32])
                    if DBG < 24:
                        continue
                    G(lambda e, s=s: e.tensor_tensor(out=mixR3[:rows, s, :], in0=tmp32[:rows, :], in1=ga3[:rows, s, :], op=ALU.mult),
                      [T_tmp32, tga], [T_mixR])
                    bank = 6 + (cnt["rms"] % 2)
                    cnt["rms"] += 1
                    pT = pb16(bank)
                    for kc in range(8):
                        src, tsrc = (mixR3, T_mixR) if kc < 4 else (mm3, tmm)
                        PE(lambda e, kc=kc, src=src, pT=pT, s=s: e.transpose(
                            out=pT[:, kc, :rows], in_=src[:rows, s, (kc % 4) * 128:(kc % 4) * 128 + 128],
                            identity=ident[:rows, :rows]), [tsrc, T_ident], [Tpb[bank]])
                    SC(lambda e, pT=pT: e.copy(out=C.hnT3[:, :, tok], in_=pT[:, :, :rows]), [Tpb[bank]], [C.T_hnT])
                proj_residual(C, h3, th, rows_list, ewout3, T_ewout, banks=(3, 4))
                store_h(dst, seq, row0, rows_list, h3, th)
            P.barrier()
            A.release(m, P)

        def phase_O(seq):
            m = A.mark()
            sq_ = str(seq)
            C = alloc_common(2)
            owin, T_owin = A.bf16("owin", 8 * 1536)
            owin3 = owin.rearrange("p (k n) -> p k n", k=8)
            LD(owin3, OWIN, T_owin, reads=[dT("OWIN")])
            ropet = [A.f32("rope%d" % i, TS * NROPE) for i in range(2)]
            zq, T_zq = A.f32("zq", 1280)
            zq3 = zq.rearrange("p (h d) -> p h d", d=64)
            sq, T_sq = A.f32("sqo", 1280)
            ss, T_ss = A.f32("sso", 20)
            tA, T_tA = A.f32("tA", 160)
            tB, T_tB = A.f32("tB", 160)
            tC, T_tC = A.f32("tC", 160)
            tD, T_tD = A.f32("tD", 160)
            c16, T_c16 = A.bf16("c16", 1280)
            c163 = c16.rearrange("p (h d) -> p h d", d=64)
            stCQ, T_stCQ = A.bf16("stCQ", 8 * NT)
            stCQ3 = stCQ.rearrange("p (a t) -> p a t", a=8)
            stCK, T_stCK = A.bf16("stCK", 2 * NT)
            stCK3 = stCK.rearrange("p (a t) -> p a t", a=2)
            cvst, T_cvst = A.bf16("cvst", TS * 4 * VS)
            cvst4 = cvst.rearrange("p (s g e) -> p s g e", s=TS, g=4)
            G(lambda e: e.memset(cvst, 1.0), [], [T_cvst])
            for ti, (row0, rows_list) in enumerate(cfg.tiles):
                nsub, ntok, nb0 = tile_info(row0, rows_list)
                h3, th = load_h(C, H3, seq, row0, rows_list)
                ra, rt = ropet[ti % 2]
                rope3 = ra.rearrange("p (s c) -> p s c", s=TS)
                LD(tm_sb(rope3, rows_list), tm_dram(ROPE, row0, rows_list), rt)
                rms_to_T(C, h3, th, rows_list)
                for s, rows in enumerate(rows_list):
                    tok = slice(s * 128, s * 128 + rows)
                    if DBG < 30:
                        continue
                    for nch in range(3):
                        for kc in range(8):
                            PE(lambda e, nch=nch, kc=kc: e.matmul(pbank[nch][:rows, :], lhsT=C.hnT3[:, kc, tok],
                                                                  rhs=owin3[:, kc, nch * 512:(nch + 1) * 512],
                                                                  start=(kc == 0), stop=(kc == 7)), [C.T_hnT, T_owin], [Tpb[nch]])
                    SC(lambda e: e.copy(out=zq[:rows, 0:512], in_=pbank[0][:rows, :]), [Tpb[0]], [T_zq])
                    SC(lambda e: e.copy(out=zq[:rows, 512:1024], in_=pbank[1][:rows, :]), [Tpb[1]], [T_zq])
                    SC(lambda e: e.copy(out=zq[:rows, 1024:1280], in_=pbank[2][:rows, 0:256]), [Tpb[2]], [T_zq])
                    V(lambda e, s=s: e.tensor_copy(out=cvst4[:rows, s, :, 0:64],
                                                   in_=pbank[2][:rows, 256:512].rearrange("p (g d) -> p g d", d=64)),
                      [Tpb[2]], [T_cvst])
                    if DBG < 31:
                        continue
                    G(lambda e: e.tensor_tensor(out=sq[:rows, :], in0=zq[:rows, :], in1=zq[:rows, :], op=ALU.mult), [T_zq], [T_sq])
                    V(lambda e: e.tensor_reduce(out=ss[:rows, :], in_=sq[:rows, :].rearrange("p (h d) -> p h d", d=64),
                                                axis=AX.X, op=ALU.add), [T_sq], [T_ss])
                    rsqrt_chain(ss[:rows, :], T_ss, 1.0 / 64.0)
                    z3 = zq3[:rows, :, :]
                    V(lambda e: e.tensor_tensor(out=z3, in0=z3, in1=ss[:rows, :].unsqueeze(2).to_broadcast([rows, 20, 64]),
                                                op=ALU.mult), [T_zq, T_ss], [T_zq])
                    G(lambda e: e.tensor_tensor(out=z3, in0=z3, in1=gck3[:rows, :, :], op=ALU.mult), [T_zq, T_gck], [T_zq])
                    if DBG < 32:
                        continue
                    xa = z3[:, :, 0:8]
                    xb = z3[:, :, 8:16]
                    cC = rope3[:rows, s, 160:168].unsqueeze(1).to_broadcast([rows, 20, 8])
                    sC = rope3[:rows, s, 168:176].unsqueeze(1).to_broadcast([rows, 20, 8])
                    o3 = c163[:rows, :, :]

                    def v3(ap):
                        return ap[:rows, :].rearrange("p (h d) -> p h d", d=8)
                    V(lambda e: e.tensor_tensor(out=v3(tA), in0=xa, in1=cC, op=ALU.mult), [T_zq, rt], [T_tA])
                    V(lambda e: e.tensor_tensor(out=v3(tB), in0=xb, in1=sC, op=ALU.mult), [T_zq, rt], [T_tB])
                    V(lambda e: e.tensor_tensor(out=o3[:, :, 0:8], in0=v3(tA), in1=v3(tB), op=ALU.subtract), [T_tA, T_tB], [T_c16])
                    G(lambda e: e.tensor_tensor(out=v3(tC), in0=xb, in1=cC, op=ALU.mult), [T_zq, rt], [T_tC])
                    G(lambda e: e.tensor_tensor(out=v3(tD), in0=xa, in1=sC, op=ALU.mult), [T_zq, rt], [T_tD])
                    G(lambda e: e.tensor_tensor(out=o3[:, :, 8:16], in0=v3(tC), in1=v3(tD), op=ALU.add), [T_tC, T_tD], [T_c16])
                    V(lambda e: e.tensor_copy(out=o3[:, :, 16:64], in_=z3[:, :, 16:64]), [T_zq], [T_c16])
                    if DBG < 33:
                        continue
                    pQ = pb16(3)
                    pK = pb16(4)
                    for a in range(10):
                        pX, bank, slot = (pQ, 3, a) if a < 8 else (pK, 4, a - 8)
                        PE(lambda e, a=a, pX=pX, slot=slot: e.transpose(out=pX[:, slot, :rows], in_=c16[:rows, a * 128:(a + 1) * 128],
                                                                        identity=ident[:rows, :rows]), [T_c16, T_ident], [Tpb[bank]])
                    SC(lambda e: e.copy(out=stCQ3[:, :, tok], in_=pQ[:, :, :rows]), [Tpb[3]], [T_stCQ])
                    V(lambda e: e.tensor_copy(out=stCK3[:, :, tok], in_=pK[:, 0:2, :rows]), [Tpb[4]], [T_stCK])
                if DBG < 34:
                    continue
                ST(CQT[seq, :, :, row0:row0 + ntok].rearrange("a p t -> p a t"), stCQ3[:, :, :ntok], T_stCQ, dT("CQT" + sq_))
                ST(CKT[seq, :, :, row0:row0 + ntok].rearrange("a p t -> p a t"), stCK3[:, :, :ntok], T_stCK, dT("CKT" + sq_))
                rws = rows_list[0]
                ST(CV[seq, 0:rws, nb0:nb0 + nsub, :, :], cvst4[:rws, :nsub, :, :], T_cvst, dT("CV" + sq_))
            P.barrier()
            A.release(m, P)

        def phase_W(seq, dst):
            m = A.mark()
            sq_ = str(seq)
            C = alloc_common(2)
            owout, T_owout = A.bf16("owout", 8 * 1024)
            owout3 = owout.rearrange("p (k n) -> p k n", k=8)
            LD(owout3, OWOUT, T_owout, reads=[dT("OWOUT")])
            mk, T_mk = A.bf16("mk", 2 * 16)
            mk3 = mk.rearrange("p (a t) -> p a t", a=2)
            mv, T_mv = A.bf16("mv", 4 * VS)
            mv3 = mv.rearrange("p (g e) -> p g e", g=4)
            LD(mk3, CKT[seq, :, :, 0:NMETA].rearrange("a p t -> p a t"), T_mk, reads=[dT("CKT" + sq_)], slow=True)
            LD(mv3[:NMETA], CV[seq, 0:NMETA, 0, :, :], T_mv, reads=[dT("CV" + sq_)])
            bufs = []
            for i in range(2):
                q, tq = A.bf16("wq%d" % i, 2 * 8 * NT)
                G(lambda e, q=q: e.memset(q, 0.0), [], [tq])
                kw, tkw = A.bf16("wk%d" % i, 2 * (NT + 256))
                vv, tvv = A.bf16("wv%d" % i, (TS + 2) * 4 * VS)
                bufs.append((q.rearrange("p (m a t) -> p m a t", m=2, a=8), tq, kw.rearrange("p (a t) -> p a t", a=2), tkw,
                             vv.rearrange("p (n g e) -> p n g e", n=TS + 2, g=4), tvv))
            pts = [A.bf16("wpt%d" % i, 512) for i in range(4)]
            mixC, T_mixC = A.bf16("mixC", TS * 1024)
            mixC3 = mixC.rearrange("p (s d) -> p s d", s=TS)
            den, T_den = A.f32("den", 4)
            es3 = esink.rearrange("p (a b) -> p a b", b=2)
            ptc = 0
            for ti, (row0, rows_list) in enumerate(cfg.tiles):
                nsub, ntok, nb0 = tile_info(row0, rows_list)
                h3, th = load_h(C, H3, seq, row0, rows_list)
                q3, tq, kw3, tkw, vw4, tvv = bufs[ti % 2]
                P.op("sync", lambda e: [e.dma_start(out=q3[0:64, 0, :, :ntok], in_=CQT[seq, :, 0:64, row0:row0 + ntok].rearrange("a p t -> p a t")),
                                        e.dma_start(out=q3[64:128, 1, :, :ntok], in_=CQT[seq, :, 64:128, row0:row0 + ntok].rearrange("a p t -> p a t"))],
                     reads=[dT("CQT" + sq_)], writes=[tq], dma_T=tq, npieces=2)
                blo = max(1, nb0 - 1)
                bhi = min(NB, nb0 + nsub)
                jlo, jhi = blo - (nb0 - 1), bhi - (nb0 - 1)
                c0 = NMETA + 128 * (blo - 1)
                c1 = NMETA + 128 * bhi
                LD(kw3[:, :, jlo * 128:(jhi + 1) * 128], CKT[seq, :, :, c0:c1].rearrange("a p t -> p a t"), tkw, reads=[dT("CKT" + sq_)])
                LD(vw4[:, jlo:jhi + 1, :, :], CV[seq, :, blo:bhi + 1, :, :], tvv, reads=[dT("CV" + sq_)])
                st_list, rest_list = [], []
                for s, rows in enumerate(rows_list):
                    tok = slice(s * 128, s * 128 + rows)
                    nb = nb0 + s
                    nbrs = [("meta", None, None)]
                    if nb == 0:
                        if NB >= 1:
                            nbrs.append((2, MNEXT[:, 112:128], "n"))
                    else:
                        if nb - 1 >= 1:
                            nbrs.append((s, MPREV[:, :], "p"))
                        nbrs.append((s + 1, None, "s"))
                        if nb + 1 <= NB:
                            nbrs.append((s + 2, MNEXT[:, :], "n"))
                    for g in range(4):
                        for ni, (slot, mask, kind) in enumerate(nbrs):
                            idx = ptc
                            ptc += 1

                            def st(s=s, rows=rows, tok=tok, g=g, ni=ni, slot=slot, idx=idx):
                                pa0 = 4 * (g // 2)
                                sb_ = (0, 1, 7)[idx % 3]
                                if slot == "meta":
                                    krows, kTap, tk_ = NMETA, mk3[:, g // 2, :], T_mk
                                else:
                                    krows, kTap, tk_ = 128, kw3[:, g // 2, slot * 128:(slot + 1) * 128], tkw
                                PE(lambda e: e.matmul(
                                    pbank[sb_][:krows, :4 * rows].rearrange("p (a i) -> p a i", a=4), lhsT=kTap,
                                    rhs=q3[:, g % 2, pa0:pa0 + 4, tok], start=True, stop=True), [tk_, tq], [Tpb[sb_]])

                            def rest(s=s, rows=rows, tok=tok, g=g, ni=ni, slot=slot, mask=mask, idx=idx, nn=len(nbrs)):
                                pa0 = 4 * (g // 2)
                                accb = 2 + (g % 2)
                                sb_ = (0, 1, 7)[idx % 3]
                                pt, tpt = pts[idx % 4]
                                if slot == "meta":
                                    krows, vap, tv_ = NMETA, mv3[:NMETA, g, 0:65], T_mv
                                else:
                                    krows, vap, tv_ = 128, vw4[:, slot, g, 0:65], tvv
                                SC(lambda e: e.activation(out=pt[:krows, :4 * rows], in_=pbank[sb_][:krows, :4 * rows],
                                                          func=AF.Exp), [Tpb[sb_]], [tpt])
                                if mask is not None:
                                    V(lambda e: e.tensor_tensor(
                                        out=pt[:, :4 * rows].rearrange("p (a i) -> p a i", a=4),
                                        in0=pt[:, :4 * rows].rearrange("p (a i) -> p a i", a=4),
                                        in1=mask.unsqueeze(1).to_broadcast([128, 4, rows]), op=ALU.mult), [tpt, T_mask], [tpt])
                                for hh in range(4):
                                    PE(lambda e, hh=hh: e.matmul(
                                        pbank[accb][:rows, hh * 65:(hh + 1) * 65], lhsT=pt[:krows, hh * rows:(hh + 1) * rows],
                                        rhs=vap, start=(ni == 0 and hh == 0), stop=(ni == nn - 1), skip_group_check=True),
                                       [tpt, tv_], [Tpb[accb]])
                                if ni != nn - 1:
                                    return
                                acc3 = pbank[accb][:rows, :260].rearrange("p (a e) -> p a e", e=65)
                                V(lambda e: e.tensor_tensor(out=den[:rows, :], in0=acc3[:, :, 64],
                                                            in1=es3[:rows, pa0:pa0 + 4, g % 2], op=ALU.add),
                                  [Tpb[accb], T_esink], [T_den])
                                V(lambda e: e.reciprocal(out=den[:rows, :], in_=den[:rows, :]), [T_den], [T_den])
                                mo = mixC3[:rows, s, :].rearrange("p (a b d) -> p a b d", b=2, d=64)[:, pa0:pa0 + 4, g % 2, :]
                                V(lambda e: e.tensor_tensor(out=mo, in0=acc3[:, :, 0:64],
                                                            in1=den[:rows, :].unsqueeze(2).to_broadcast([rows, 4, 64]),
                                                            op=ALU.mult), [Tpb[accb], T_den], [T_mixC])
                                if g != 3:
                                    return
                                bank = 6
                                pT = pb16(bank)
                                for kc in range(8):
                                    PE(lambda e, kc=kc: e.transpose(out=pT[:, kc, :rows], in_=mixC3[:rows, s, kc * 128:(kc + 1) * 128],
                                                                    identity=ident[:rows, :rows]), [T_mixC, T_ident], [Tpb[bank]])
                                SC(lambda e: e.copy(out=C.hnT3[:, :, tok], in_=pT[:, :, :rows]), [Tpb[bank]], [C.T_hnT])
                            st_list.append(st)
                            rest_list.append(rest)
                st_list[0]()
                if len(st_list) > 1:
                    st_list[1]()
                for i_ in range(len(st_list)):
                    if i_ + 2 < len(st_list):
                        st_list[i_ + 2]()
                    rest_list[i_]()
                proj_residual(C, h3, th, rows_list, owout3, T_owout, banks=(4, 5))
                store_h(dst, seq, row0, rows_list, h3, th)
            P.barrier()
            A.release(m, P)

        for seq in range(NSEQ):
            if upto == 1:
                phase_F(seq, "x", "y", [0])
                continue
            phase_F(seq, "x", H1, [0])
            if upto == 10:
                continue
            phase_E(seq)
            if upto == 11:
                if DBG == 4.5:
                    mdbg = A.mark()
                    dbg32, T_dbg32 = A.f32("dbg32", 1024)
                    V(lambda e: e.tensor_copy(out=dbg32, in_=wkvb), [T_wkvb], [T_dbg32])
                    ST(y_out[seq, 0:128, :], dbg32, T_dbg32, dT("y"))
                    P.barrier()
                    A.release(mdbg, P)
                continue
            phase_scan(seq)
            if upto == 12:
                continue
            phase_M(seq)
            if upto == 13:
                continue
            if upto == 2:
                phase_R(seq, "y")
                continue
            phase_R(seq, H2)
            if upto == 3:
                phase_F(seq, H2, "y", [1])
                continue
            if upto == 4:
                phase_F(seq, H2, "y", [1, 2])
                continue
            phase_F(seq, H2, H3, [1, 2])
            phase_O(seq)
            if upto == 14:
                continue
            if upto == 5:
                phase_W(seq, "y")
                continue
            phase_W(seq, H4)
            phase_F(seq, H4, "y", [3])
        P.barrier()
        P.finalize()
        print("ops/waits per engine:", P.counts, "sems:", P.nsem, "arena peak KB:", A.peak * 4 / 1024)
    return nc


PARAM_KEYS = ("meta_tokens", "ffn_norm", "ffn_w_gate", "ffn_w_up", "ffn_w_down", "mix_norm", "even_w_in",
              "ret_decay_f", "ret_decay_b", "ret_out_norm", "mla_q_norm", "mla_w_qb", "mla_kv_norm", "mla_w_kvb",
              "mla_qk_norm_q", "mla_qk_norm_k", "even_w_out", "odd_w_in", "swa_q_norm", "swa_k_norm", "swa_sink",
              "odd_w_out")


def make_core_inputs(params, x_core):
    L = x_core.shape[1] + NMETA
    c, cm, rope = host_constants(L)
    m = {"x": np.ascontiguousarray(x_core, dtype=np.float32), "CONST": c, "CONSTM": cm, "ROPE": rope}
    for k in PARAM_KEYS:
        m[k] = np.ascontiguousarray(params[k], dtype=np.float32)
    return m


def kernel(**inputs):
    xp = np.asarray(inputs["x_prompt"])
    xs = np.asarray(inputs["x_sample"])
    params = {k: np.asarray(inputs[k]) for k in PARAM_KEYS}
    n = 8
    S = xp.shape[1]
    cfg = Cfg(S=S, NSEQ=3)
    nc = build(cfg)
    in_maps = []
    for c in range(n):
        xc = np.stack([xp[2 * c], xp[2 * c + 1], xs[c]], axis=0)
        in_maps.append(make_core_inputs(params, xc))
    res = run_bass_kernel_spmd(nc, in_maps, core_ids=list(range(n)))
    yp = np.empty_like(xp)
    ys = np.empty_like(xs)
    for c in range(n):
        y = res.results[c]["y"]
        yp[2 * c] = y[0]
        yp[2 * c + 1] = y[1]
        ys[c] = y[2]
    return (yp, ys)
```
